# Optimizing a Trainium2 kernel written in Bass

```python
import math
import jax, jax.numpy as jnp
from jax import lax
import numpy as np

D_MODEL = 1024
BATCH = 4
SEQ = 4096
DEPTH = 2

HEAD_DIM = 64
HEADS_A = 8
HEADS_B = 8
HEADS_C = 8
WIDTH = HEADS_A * HEAD_DIM
N_BRANCH = 3
ROPE_DIM = HEAD_DIM // 4
ROPE_THETA = 500000.0
MOBA_BLOCK = 256
MOBA_TOPK = 3
MOBA_Q_CHUNK = 32
IDX_HEADS = 8
IDX_DIM = 64
DSA_TOPK_MAX = 256
Q_BLOCK = 128
RMS_EPS = 1e-6

SECTION_SIZES = (
    WIDTH, WIDTH, WIDTH, WIDTH,
    WIDTH, WIDTH, WIDTH, WIDTH,
    WIDTH, WIDTH, WIDTH, WIDTH,
    IDX_HEADS * IDX_DIM, IDX_DIM, IDX_HEADS,
    HEADS_C,
    N_BRANCH * D_MODEL,
)
D_IN = sum(SECTION_SIZES)
SPLIT_POINTS = tuple(int(v) for v in np.cumsum(SECTION_SIZES)[:-1])

kernel_name = "hybrid_moba_dsa_fox_gated_trunk"


def rms_norm(x, gain):
    xf = x.astype(jnp.float32)
    y = xf * lax.rsqrt(jnp.mean(xf * xf, axis=-1, keepdims=True) + RMS_EPS)
    return (y * gain.astype(jnp.float32)).astype(x.dtype)


def rope_partial(t, pos):
    half = ROPE_DIM // 2
    inv_freq = jnp.power(jnp.float32(ROPE_THETA), -jnp.arange(0, ROPE_DIM, 2, dtype=jnp.float32) / ROPE_DIM)
    ang = pos.astype(jnp.float32)[:, None] * inv_freq[None, :]
    cos = jnp.cos(ang)[None, :, None, :]
    sin = jnp.sin(ang)[None, :, None, :]
    tr = t[..., :ROPE_DIM].astype(jnp.float32)
    t1, t2 = tr[..., :half], tr[..., half:]
    rot = jnp.concatenate([t1 * cos - t2 * sin, t2 * cos + t1 * sin], axis=-1).astype(t.dtype)
    return jnp.concatenate([rot, t[..., ROPE_DIM:]], axis=-1)


def moba_attention(q, k, v):
    B, S, H, d = q.shape
    n_blk = -(-S // MOBA_BLOCK)
    pad = n_blk * MOBA_BLOCK - S
    kp = jnp.pad(k, ((0, 0), (0, pad), (0, 0), (0, 0)))
    vp = jnp.pad(v, ((0, 0), (0, pad), (0, 0), (0, 0)))
    kb = kp.reshape(B, n_blk, MOBA_BLOCK, H, d).transpose(0, 3, 1, 2, 4)
    vb = vp.reshape(B, n_blk, MOBA_BLOCK, H, d).transpose(0, 3, 1, 2, 4)
    k_mean = jnp.mean(kb, axis=3)
    qh = q.transpose(0, 2, 1, 3)
    k_sel = min(MOBA_TOPK, n_blk - 1)
    scale = d ** -0.5
    bi = jnp.arange(B)[:, None, None, None]
    hi = jnp.arange(H)[None, :, None, None]

    def chunk(ci):
        start = ci * MOBA_Q_CHUNK
        qc = lax.dynamic_slice_in_dim(qh, start, MOBA_Q_CHUNK, axis=2)
        qpos = start + jnp.arange(MOBA_Q_CHUNK)
        own = start // MOBA_BLOCK
        k_own = lax.dynamic_index_in_dim(kb, own, axis=2, keepdims=False)
        v_own = lax.dynamic_index_in_dim(vb, own, axis=2, keepdims=False)
        kpos_own = own * MOBA_BLOCK + jnp.arange(MOBA_BLOCK)
        s_own = jnp.einsum('bhqd,bhkd->bhqk', qc, k_own).astype(jnp.float32) * scale
        s_own = jnp.where(kpos_own[None, :] <= qpos[:, None], s_own, -jnp.inf)
        if k_sel == 0:
            p = jax.nn.softmax(s_own, axis=-1).astype(v.dtype)
            return jnp.einsum('bhqk,bhkd->bhqd', p, v_own)
        gate = jnp.einsum('bhqd,bhnd->bhqn', qc, k_mean).astype(jnp.float32)
        gate = jnp.where((jnp.arange(n_blk) < own)[None, None, None, :], gate, -jnp.inf)
        gval, gidx = lax.top_k(gate, k_sel)
        ok = gval > -jnp.inf
        k_g = kb[bi, hi, gidx]
        v_g = vb[bi, hi, gidx]
        s_past = jnp.einsum('bhqd,bhqnkd->bhqnk', qc, k_g).astype(jnp.float32) * scale
        s_past = jnp.where(ok[..., None], s_past, -jnp.inf)
        s_past = s_past.reshape(B, H, MOBA_Q_CHUNK, k_sel * MOBA_BLOCK)
        p = jax.nn.softmax(jnp.concatenate([s_past, s_own], axis=-1), axis=-1).astype(v.dtype)
        p_past = p[..., :k_sel * MOBA_BLOCK].reshape(B, H, MOBA_Q_CHUNK, k_sel, MOBA_BLOCK)
        p_own = p[..., k_sel * MOBA_BLOCK:]
        return (jnp.einsum('bhqnk,bhqnkd->bhqd', p_past, v_g)
                + jnp.einsum('bhqk,bhkd->bhqd', p_own, v_own))

    out = lax.map(chunk, jnp.arange(S // MOBA_Q_CHUNK))
    return out.transpose(1, 0, 3, 2, 4).reshape(B, S, H, d)


def dsa_attention(q, k, v, q_idx, k_idx, w_idx):
    B, S, H, d = q.shape
    top = min(DSA_TOPK_MAX, S // 4)
    scale = d ** -0.5
    bi = jnp.arange(B)[:, None, None]
    kpos = jnp.arange(S)

    def chunk(ci):
        start = ci * Q_BLOCK
        qpos = start + jnp.arange(Q_BLOCK)
        qi = lax.dynamic_slice_in_dim(q_idx, start, Q_BLOCK, axis=1)
        wi = lax.dynamic_slice_in_dim(w_idx, start, Q_BLOCK, axis=1)
        rel = jax.nn.relu(jnp.einsum('bqhd,bsd->bqhs', qi, k_idx))
        isc = jnp.einsum('bqh,bqhs->bqs', wi, rel).astype(jnp.float32)
        isc = jnp.where(kpos[None, None, :] <= qpos[None, :, None], isc, -jnp.inf)
        ival, idx = lax.top_k(isc, top)
        ok = ival > -jnp.inf
        k_g = k[bi, idx]
        v_g = v[bi, idx]
        qc = lax.dynamic_slice_in_dim(q, start, Q_BLOCK, axis=1)
        s = jnp.einsum('bqhd,bqkhd->bhqk', qc, k_g).astype(jnp.float32) * scale
        s = jnp.where(ok[:, None], s, -jnp.inf)
        p = jax.nn.softmax(s, axis=-1).astype(v.dtype)
        return jnp.einsum('bhqk,bqkhd->bqhd', p, v_g)

    out = lax.map(chunk, jnp.arange(S // Q_BLOCK))
    return out.transpose(1, 0, 2, 3, 4).reshape(B, S, H, d)


def forgetting_attention(q, k, v, log_f):
    B, S, H, d = q.shape
    scale = d ** -0.5
    csum = jnp.cumsum(log_f, axis=1).transpose(0, 2, 1)
    qh = q.transpose(0, 2, 1, 3)
    kh = k.transpose(0, 2, 1, 3)
    vh = v.transpose(0, 2, 1, 3)
    kpos = jnp.arange(S)

    def chunk(ci):
        start = ci * Q_BLOCK
        qpos = start + jnp.arange(Q_BLOCK)
        qc = lax.dynamic_slice_in_dim(qh, start, Q_BLOCK, axis=2)
        cq = lax.dynamic_slice_in_dim(csum, start, Q_BLOCK, axis=2)
        s = jnp.einsum('bhqd,bhkd->bhqk', qc, kh).astype(jnp.float32) * scale
        s = s + (cq[..., :, None] - csum[:, :, None, :])
        s = jnp.where(kpos[None, :] <= qpos[:, None], s, -jnp.inf)
        p = jax.nn.softmax(s, axis=-1).astype(v.dtype)
        return jnp.einsum('bhqk,bhkd->bhqd', p, vh)

    out = lax.map(chunk, jnp.arange(S // Q_BLOCK))
    return out.transpose(1, 0, 3, 2, 4).reshape(B, S, H, d)


def hybrid_layer(x, gain, w_in, f_bias, w_branch, w_out):
    B, S, _ = x.shape
    pos = jnp.arange(S)
    h = rms_norm(x, gain)
    proj = jnp.einsum('bsd,de->bse', h, w_in)
    (qa, ka, va, ga, qb, kb_, vb_, gb, qc, kc, vc, gc,
     q_idx, k_idx, w_idx, f_logit, merge_logit) = jnp.split(proj, SPLIT_POINTS, axis=-1)

    def heads(t, n):
        return t.reshape(B, S, n, HEAD_DIM)

    oa = moba_attention(rope_partial(heads(qa, HEADS_A), pos), rope_partial(heads(ka, HEADS_A), pos),
                        heads(va, HEADS_A)).reshape(B, S, WIDTH) * jax.nn.silu(ga)
    qi = rope_partial(q_idx.reshape(B, S, IDX_HEADS, IDX_DIM), pos)
    ki = rope_partial(k_idx[:, :, None, :], pos)[:, :, 0, :]
    wi = w_idx * (IDX_HEADS ** -0.5 * IDX_DIM ** -0.5)
    ob = dsa_attention(rope_partial(heads(qb, HEADS_B), pos), rope_partial(heads(kb_, HEADS_B), pos),
                       heads(vb_, HEADS_B), qi, ki, wi).reshape(B, S, WIDTH) * jax.nn.silu(gb)
    log_f = jax.nn.log_sigmoid((f_logit + f_bias).astype(jnp.float32))
    oc = forgetting_attention(heads(qc, HEADS_C), heads(kc, HEADS_C), heads(vc, HEADS_C),
                              log_f).reshape(B, S, WIDTH) * jax.nn.silu(gc)

    y = jnp.einsum('nbsw,nwd->bsnd', jnp.stack([oa, ob, oc], axis=0), w_branch)
    gates = jax.nn.sigmoid(merge_logit).reshape(B, S, N_BRANCH, D_MODEL)
    merged = jnp.sum(gates * y, axis=2)
    return x + jnp.einsum('bsd,de->bse', merged, w_out)


def setup_inputs(seed: int = 0) -> dict:
    key = jax.random.key(seed)
    ks = jax.random.split(key, 8)
    x = jax.random.normal(ks[0], (BATCH, SEQ, D_MODEL), jnp.float32)
    norm_gain = 1.0 + 0.05 * jax.random.normal(ks[1], (DEPTH, D_MODEL), jnp.float32)
    w_in = jax.random.normal(ks[2], (DEPTH, D_MODEL, D_IN), jnp.float32) * D_MODEL ** -0.5
    forget_bias = 3.0 + 0.5 * jax.random.normal(ks[3], (DEPTH, HEADS_C), jnp.float32)
    w_branch = jax.random.normal(ks[4], (DEPTH, N_BRANCH, WIDTH, D_MODEL), jnp.float32) * WIDTH ** -0.5
    w_out = jax.random.normal(ks[5], (DEPTH, D_MODEL, D_MODEL), jnp.float32) * (0.5 * D_MODEL ** -0.5)
    final_gain = 1.0 + 0.05 * jax.random.normal(ks[6], (D_MODEL,), jnp.float32)
    return {"x": x, "norm_gain": norm_gain, "w_in": w_in, "forget_bias": forget_bias,
            "w_branch": w_branch, "w_out": w_out, "final_gain": final_gain}


def reference(x, norm_gain, w_in, forget_bias, w_branch, w_out, final_gain):
    h = x
    for layer in range(DEPTH):
        h = hybrid_layer(h, norm_gain[layer], w_in[layer], forget_bias[layer],
                         w_branch[layer], w_out[layer])
    return rms_norm(h, final_gain)
```

```python
import numpy as np
from contextlib import ExitStack
import concourse.bass as bass
import concourse.mybir as mybir
from concourse.bass_utils import run_bass_kernel_spmd

F32 = mybir.dt.float32
BF16 = mybir.dt.bfloat16
ALU = mybir.AluOpType
AF = mybir.ActivationFunctionType
AX = mybir.AxisListType

D = 1024
S_LEN = 4096
DEPTH = 2
DIN = 9808
NT = S_LEN // 128
NIT = 22
NEG = -30000.0
NEGBIG = -1.0e30
RMS_EPS = 1e-6
C_QA, C_KA, C_VA, C_GA = 0, 512, 1024, 1536
C_QB, C_KB, C_VB, C_GB = 2048, 2560, 3072, 3584
C_QC, C_KC, C_VC, C_GC = 4096, 4608, 5120, 5632
C_QI, C_KI, C_WI, C_F, C_M = 6144, 6656, 6720, 6728, 6736
K_ID, K_TRI, K_TRN, K_SEL, K_COS, K_SIN, K_POW, K_GAIN, K_FB = 0, 128, 256, 384, 512, 768, 1024, 1056, 1080
NCST = 1096


class Buf:
    __slots__ = ("name", "w", "r", "excl")

    def __init__(self, name, excl=False):
        self.name = name
        self.w = None
        self.r = []
        self.excl = excl


class Op:
    __slots__ = ("eng", "fn", "deps", "sig", "sem", "val", "dma")


ENGS = ("tensor", "vector", "scalar", "gpsimd", "sync")
DMA_POOL = 10


class Sched:
    def __init__(self):
        self.ops = {e: [] for e in ENGS}
        self.bar = None
        self.bar_seen = set()

    def add(self, eng, fn, reads=(), writes=(), dma=False):
        op = Op()
        op.eng = eng
        op.fn = fn
        op.dma = dma
        op.sig = dma
        op.sem = None
        op.val = 0
        deps = {}
        for b in reads:
            if b.w is not None:
                deps[id(b.w)] = b.w
            if b.excl:
                for r in b.r:
                    if r.eng != eng:
                        deps[id(r)] = r
        for b in writes:
            if b.w is not None:
                deps[id(b.w)] = b.w
            for r in b.r:
                deps[id(r)] = r
        if self.bar is not None and eng not in self.bar_seen:
            self.bar_seen.add(eng)
            for d in self.bar:
                deps[id(d)] = d
        for b in reads:
            b.r.append(op)
        for b in writes:
            b.w = op
            b.r = []
        dl = []
        for d in deps.values():
            if d is op:
                continue
            if (not d.dma) and (not dma) and d.eng == "tensor" and eng == "tensor":
                continue
            d.sig = True
            dl.append(d)
        op.deps = dl
        self.ops[eng].append(op)
        return op

    def barrier(self):
        deps = []
        for e in ENGS:
            got_c = False
            nd = 0
            for op in reversed(self.ops[e]):
                if op.dma:
                    if nd < DMA_POOL:
                        deps.append(op)
                        nd += 1
                elif not got_c:
                    got_c = True
                    deps.append(op)
                if got_c and nd >= DMA_POOL:
                    break
        self.bar = deps
        self.bar_seen = set()

    def emit(self, nc, stack, final_bufs=()):
        fin = []
        for b in final_bufs:
            if b.w is not None:
                b.w.sig = True
                fin.append(b.w)
        esem = {e: stack.enter_context(nc.semaphore("c_" + e)) for e in ENGS if e != "sync"}
        pools = {e: [stack.enter_context(nc.semaphore("d_%s_%d" % (e, i))) for i in range(DMA_POOL)]
                 for e in ("sync", "scalar", "gpsimd")}
        for e in ENGS:
            cnt = 0
            k = 0
            uses = [0] * DMA_POOL
            last = [None] * DMA_POOL
            for op in self.ops[e]:
                if op.dma:
                    j = k % DMA_POOL
                    k += 1
                    uses[j] += 1
                    op.sem = pools[e][j]
                    op.val = 16 * uses[j]
                    if last[j] is not None:
                        op.deps.append(last[j])
                    last[j] = op
                elif op.sig:
                    cnt += 1
                    op.sem = esem[e]
                    op.val = cnt
        block = stack.enter_context(nc.Block())

        def run(e, eng):
            known = {}
            for op in self.ops[e]:
                for d in op.deps:
                    key = d.sem.num
                    if known.get(key, 0) < d.val:
                        eng.wait_ge(d.sem, d.val)
                        known[key] = d.val
                ins = op.fn(eng)
                if op.dma:
                    ins.then_inc(op.sem, 16)
                elif op.sig:
                    ins.then_inc(op.sem, 1)
            if e == "sync":
                for d in fin:
                    if known.get(d.sem.num, 0) < d.val:
                        eng.wait_ge(d.sem, d.val)
                        known[d.sem.num] = d.val

        @block.tensor
        def _(eng):
            run("tensor", eng)

        @block.vector
        def _(eng):
            run("vector", eng)

        @block.scalar
        def _(eng):
            run("scalar", eng)

        @block.gpsimd
        def _(eng):
            run("gpsimd", eng)

        @block.sync
        def _(eng):
            run("sync", eng)


def build_program(n_layers=DEPTH, debug=False, branches=(0, 1, 2), stop=None):
    nc = bass.Bass("TRN2", target_bir_lowering=False)
    xT = nc.dram_tensor("xT", [D, S_LEN], F32, kind="ExternalInput").ap()
    used_inputs = ["xT", "cst", "onehot"]
    nc.used_inputs = used_inputs
    _wcache = {}

    def wten(name, shape):
        if name not in _wcache:
            _wcache[name] = nc.dram_tensor(name, shape, F32, kind="ExternalInput").ap()
            used_inputs.append(name)
        return _wcache[name]
    cst_d = nc.dram_tensor("cst", [128, NCST], F32, kind="ExternalInput").ap()
    oh_d = nc.dram_tensor("onehot", [16, S_LEN], F32, kind="ExternalInput").ap()
    outT = nc.dram_tensor("outT", [D, S_LEN], F32, kind="ExternalOutput").ap()
    dbg_kind = "ExternalOutput" if debug else "Internal"
    xs = nc.dram_tensor("xs", [D, S_LEN], F32, kind=dbg_kind).ap()
    hs = nc.dram_tensor("hs", [D, S_LEN], BF16, kind="Internal").ap()
    og_d = nc.dram_tensor("og", [3, 512, S_LEN], BF16, kind=dbg_kind).ap()
    dbgf = nc.dram_tensor("dbgf", [128, 4, 256], F32, kind="ExternalOutput").ap() if debug else None

    S = Sched()
    st = ExitStack()
    with st:
        def sbt(name, shape, dt):
            return st.enter_context(nc.sbuf_tensor(name, shape, dt))

        def pst(name, shape, dt):
            return st.enter_context(nc.psum_tensor(name, shape, dt))

        cst = sbt("cst_sb", [128, NCST], F32)
        identb = sbt("identb", [128, 128], BF16)
        trib = sbt("trib", [128, 128], BF16)
        onesb = sbt("onesb", [128, 128], BF16)
        hc = [sbt("hc%d" % i, [128, 8, 512], BF16) for i in range(2)]
        sbT = [sbt("sbT%d" % i, [128, 512], BF16) for i in range(2)]
        rtmp = [sbt("rtmp%d" % i, [128, 64], F32) for i in range(2)]
        pbuf = [sbt("pbuf%d" % i, [128, 512], BF16) for i in range(4)]
        ARENA_B = 168 * 1024
        arena = sbt("arena", [128, ARENA_B // 2], BF16)
        B_cst = Buf("cst")
        B_hc = [Buf("hc0"), Buf("hc1")]
        B_sbT = [Buf("sbT0"), Buf("sbT1")]
        B_rtmp = [Buf("rt0"), Buf("rt1")]
        B_pbuf = [Buf("pb%d" % i) for i in range(4)]
        B_final = Buf("final")
        identf = cst[:, K_ID:K_ID + 128]
        trinegf = cst[:, K_TRN:K_TRN + 128]
        sel127f = cst[:, K_SEL:K_SEL + 128]
        cos_t = cst[:, K_COS:K_COS + 256].rearrange("p (a b) -> p a b", b=8)
        sin_t = cst[:, K_SIN:K_SIN + 256].rearrange("p (a b) -> p a b", b=8)
        pow_t = cst[:, K_POW:K_POW + NIT]
        gain_t = cst[:, K_GAIN:K_GAIN + 24].rearrange("p (a b) -> p a b", b=8)
        fb_t = cst[:, K_FB:K_FB + 16].rearrange("p (a b) -> p a b", b=8)

        PS = [pst("ps%d" % i, [128, 512], F32) for i in range(7)]
        B_PS = [Buf("ps%d" % i, excl=True) for i in range(7)]
        PSB = pst("psb", [128, 1024], BF16)
        B_PSB = Buf("psb", excl=True)

        class Carver:
            def __init__(self):
                self.off = 0

            def take(self, nbytes, dt, pat=None, **kw):
                nbytes = (nbytes + 63) // 64 * 64
                a = arena[:, self.off // 2:(self.off + nbytes) // 2]
                self.off += nbytes
                assert self.off <= ARENA_B, ("arena overflow", self.off)
                if dt == F32:
                    a = a.bitcast(F32)
                if pat is not None:
                    a = a.rearrange(pat, **kw)
                return a

        S.add("sync", lambda e: e.dma_start(out=cst[:], in_=cst_d), writes=[B_cst], dma=True)
        B_ib, B_tb, B_ob = Buf("identb"), Buf("trib"), Buf("onesb")
        S.add("vector", lambda e: e.tensor_copy(identb[:], cst[:, K_ID:K_ID + 128]), reads=[B_cst], writes=[B_ib])
        S.add("vector", lambda e: e.tensor_copy(trib[:], cst[:, K_TRI:K_TRI + 128]), reads=[B_cst], writes=[B_tb])
        S.add("vector", lambda e: e.memset(onesb[:], 1.0), writes=[B_ob])

        rr = {"ps": 0, "acc": 0, "sbT": 0, "hc": 0, "pb": 0}

        def next_ps():
            i = rr["ps"] % 5
            rr["ps"] += 1
            return PS[i], B_PS[i]

        def next_acc():
            i = 5 + rr["acc"] % 2
            rr["acc"] += 1
            return PS[i], B_PS[i]

        def load_w(dst, B_dst, src_ap, eng="gpsimd"):
            S.add(eng, lambda e: e.dma_start(out=dst, in_=src_ap), writes=[B_dst], dma=True)

        def w_in_view(l, c0, n):
            return wten("w_in%d" % l, [D, DIN])[:, c0:c0 + n].rearrange("(k p) n -> p k n", p=128)

        def proj_T(h, B_h, tcol, w, B_w, ncols):
            ps, B = next_ps()
            for k in range(8):
                S.add("tensor", lambda e, k=k: e.matmul(ps[:, 0:ncols], h[:, k, tcol:tcol + 128], w[:, k, 0:ncols],
                                                        start=(k == 0), stop=(k == 7)),
                      reads=[B_h, B_w], writes=[B])
            return ps, B

        def evac_T(ps, B, ncols, func=None, nheads=8, rope_tt=None, dup=False):
            i = rr["sbT"] % 2
            rr["sbT"] += 1
            sb, Bs = sbT[i], B_sbT[i]
            if func is None:
                S.add("scalar", lambda e: e.copy(sb[:, 0:ncols], ps[:, 0:ncols]), reads=[B], writes=[Bs])
            else:
                S.add("scalar", lambda e: e.activation(sb[:, 0:ncols], ps[:, 0:ncols], func), reads=[B], writes=[Bs])
            if rope_tt is not None:
                p3 = ps[:, 0:ncols].rearrange("p (h d) -> p d h", d=64)
                s3 = sb[:, 0:ncols].rearrange("p (h d) -> p d h", d=64)
                cb = cos_t[:, rope_tt, :].unsqueeze(2).to_broadcast([128, 8, nheads])
                sn = sin_t[:, rope_tt, :].unsqueeze(2).to_broadcast([128, 8, nheads])
                ta = rtmp[0][:, 0:nheads * 8].rearrange("p (d h) -> p d h", d=8)
                tb = rtmp[1][:, 0:nheads * 8].rearrange("p (d h) -> p d h", d=8)
                x1, x2 = p3[:, 0:8, :], p3[:, 8:16, :]
                Ba, Bb = B_rtmp
                S.add("vector", lambda e: e.tensor_tensor(ta, x1, cb, ALU.mult), reads=[B, B_cst], writes=[Ba])
                S.add("vector", lambda e: e.tensor_tensor(tb, x2, sn, ALU.mult), reads=[B, B_cst], writes=[Bb])
                S.add("vector", lambda e: e.tensor_tensor(s3[:, 0:8, :], ta, tb, ALU.subtract), reads=[Ba, Bb], writes=[Bs])
                S.add("vector", lambda e: e.tensor_tensor(ta, x2, cb, ALU.mult), reads=[B, B_cst], writes=[Ba])
                S.add("vector", lambda e: e.tensor_tensor(tb, x1, sn, ALU.mult), reads=[B, B_cst], writes=[Bb])
                S.add("vector", lambda e: e.tensor_tensor(s3[:, 8:16, :], ta, tb, ALU.add), reads=[Ba, Bb], writes=[Bs])
            if dup:
                S.add("vector", lambda e: e.tensor_copy(sb[:, 64:128], sb[:, 0:64]), reads=[Bs], writes=[Bs])
            return sb, Bs

        def to_F(sb, Bs, dst3, B_dst, width, nparts, eng="vector"):
            ng = len(width)
            for g, (c0, cw) in enumerate(width):
                S.add("tensor", lambda e, g=g, c0=c0, cw=cw: e.transpose(PSB[0:cw, g * 128:(g + 1) * 128], sb[:, c0:c0 + cw], identb[:]),
                      reads=[Bs, B_ib], writes=[B_PSB])
            src = PSB[0:nparts, 0:ng * 128].rearrange("p (g t) -> p g t", t=128)
            if eng == "vector":
                S.add("vector", lambda e: e.tensor_copy(dst3, src), reads=[B_PSB], writes=[B_dst])
            else:
                S.add("scalar", lambda e: e.copy(dst3, src), reads=[B_PSB], writes=[B_dst])

        G64 = [(h * 64, 64) for h in range(8)]
        G128 = [(p * 128, 128) for p in range(4)]

        def norm_chunk(xc, B_xc, gi, sq, B_sq, rs, B_rs, out3, B_out):
            S.add("scalar", lambda e: e.activation(sq[:].rearrange("p a b -> p (a b)"), xc[:].rearrange("p a b -> p (a b)"), AF.Square),
                  reads=[B_xc], writes=[B_sq])
            ps, B = next_ps()
            for k in range(8):
                S.add("tensor", lambda e, k=k: e.matmul(ps[:], onesb[:], sq[:, k, :], start=(k == 0), stop=(k == 7)),
                      reads=[B_sq, B_ob], writes=[B])
            S.add("scalar", lambda e: e.activation(rs[:], ps[:], AF.Sqrt, bias=eps_t[:, 0:1], scale=1.0 / D), reads=[B, B_eps], writes=[B_rs])
            S.add("vector", lambda e: e.reciprocal(rs[:], rs[:]), reads=[B_rs], writes=[B_rs])
            for k in range(8):
                S.add("vector", lambda e, k=k: e.scalar_tensor_tensor(out3[:, k, :], xc[:, k, :], gain_t[:, gi, k:k + 1], rs[:],
                                                                      ALU.mult, ALU.mult),
                      reads=[B_xc, B_rs, B_cst], writes=[B_out])

        eps_t = sbt("eps_t", [128, 2], F32)
        B_eps = Buf("eps")
        S.add("vector", lambda e: e.memset(eps_t[:, 0:1], RMS_EPS), writes=[B_eps])
        S.add("vector", lambda e: e.memset(eps_t[:, 1:2], 1.0), writes=[B_eps])

        def phase0():
            cv = Carver()
            xc = [cv.take(16384, F32, "p (a b) -> p a b", b=512) for _ in range(2)]
            B_xc = [Buf("xc0"), Buf("xc1")]
            sq = cv.take(8192, BF16, "p (a b) -> p a b", b=512)
            B_sq = Buf("sq")
            rs = cv.take(2048, F32)
            B_rs = Buf("rs")
            import os
            for ci in range(int(os.environ.get('P0N', '8'))):
                x_, Bx = xc[ci % 2], B_xc[ci % 2]
                S.add("sync", lambda e, x_=x_, ci=ci: e.dma_start(out=x_, in_=xT[:, ci * 512:(ci + 1) * 512].rearrange("(k p) t -> p k t", p=128)),
                      writes=[Bx], dma=True)
                if os.environ.get('P0XS', '1') == '1':
                    S.add("gpsimd", lambda e, x_=x_, ci=ci: e.dma_start(out=xs[:, ci * 512:(ci + 1) * 512].rearrange("(k p) t -> p k t", p=128), in_=x_),
                          reads=[Bx], writes=[B_xs[ci]], dma=True)
                h_, Bh = hc[ci % 2], B_hc[ci % 2]
                norm_chunk(x_, Bx, 0, sq, B_sq, rs, B_rs, h_, Bh)
                S.add("sync", lambda e, h_=h_, ci=ci: e.dma_start(out=hs[:, ci * 512:(ci + 1) * 512].rearrange("(k p) t -> p k t", p=128), in_=h_[:]),
                      reads=[Bh], writes=[B_hs[ci]], dma=True)

        B_xs = [Buf("xs%d" % i) for i in range(8)]
        B_hs = [Buf("hs%d" % i) for i in range(8)]
        B_og = [[Buf("og%d_%d" % (n, i)) for i in range(8)] for n in range(3)]

        def hs_bufs(c0, width):
            return [B_hs[i] for i in range(c0 // 512, (c0 + width - 1) // 512 + 1)]

        def load_hc_dep(c0, width):
            i = rr["hc"] % 2
            rr["hc"] += 1
            h, B = hc[i], B_hc[i]
            S.add("sync", lambda e: e.dma_start(out=h[:, :, 0:width],
                                                in_=hs[:, c0:c0 + width].rearrange("(k p) t -> p k t", p=128)),
                  reads=hs_bufs(c0, width), writes=[B], dma=True)
            return h, B

        def branch(l, n):
            S.barrier()
            cv = Carver()
            moba, dsa, fox = (n == 0), (n == 1), (n == 2)
            CW = 256 if dsa else 512
            NJ = CW // 128
            NCH = S_LEN // CW
            cq, ck, cvv, cg = [(C_QA, C_KA, C_VA, C_GA), (C_QB, C_KB, C_VB, C_GB), (C_QC, C_KC, C_VC, C_GC)][n]
            if moba:
                KT = cv.take(8 * S_LEN * 2, BF16, "p (h s) -> p h s", h=8)
            else:
                KT = cv.take(4 * S_LEN * 2, BF16, "p (h s) -> p h s", h=4)
            B_KT = [Buf("KT%d" % i) for i in range(NT)]
            V = cv.take(NT * 8 * 65 * 2, BF16, "p (s h d) -> p s h d", s=NT, h=8)
            B_V = [Buf("V%d" % i) for i in range(NT)]
            B_Vones = Buf("Vones")
            W = [cv.take(8 * 512 * 2, BF16, "p (k c) -> p k c", k=8) for _ in range(3)]
            B_W = [Buf("W%d" % i) for i in range(3)]
            nqh = 8 if moba else 4
            QT = cv.take(nqh * CW * 2, BF16, "p (h t) -> p h t", h=nqh)
            B_QT = Buf("QT")
            B_QTa = Buf("QTaug")
            SG = cv.take(8 * CW * 2, BF16, "p (h t) -> p h t", h=8)
            B_SG = Buf("SG")
            OG = cv.take(8 * CW * 2, BF16, "p (h t) -> p h t", h=8)
            B_OG = Buf("OG")
            bc_sb = cv.take(CW * 4, F32)
            B_bc = Buf("bc")
            otmp = cv.take(CW * 4, F32)
            B_ot = Buf("otmp")
            rcb = cv.take(CW * 2, BF16)
            B_rcb = Buf("rcb")
            rcf = cv.take(CW * 4, F32)
            B_rcf = Buf("rcf")

            S.add("gpsimd", lambda e: e.memset(V.rearrange("p s h d -> p (s h) d")[:, :, 64:65], 1.0), writes=[B_Vones])

            load_w(W[0][:], B_W[0], w_in_view(l, ck, 512))
            load_w(W[1][:], B_W[1], w_in_view(l, cvv, 512))
            B_KTa = Buf("KTaug")
            if moba:
                for h in range(8):
                    S.add("gpsimd", lambda e, h=h: e.dma_start(out=KT[64:80, h, :], in_=oh_d), writes=[B_KTa], dma=True)
            if dsa:
                Wi = cv.take(8 * 80 * 2, BF16, "p (k c) -> p k c", k=8)
                B_Wi = Buf("Wi")
                load_w(Wi[:, :, 0:72], B_Wi, w_in_view(l, C_KI, 72))
                KiT = cv.take(S_LEN * 2, BF16)
                B_KiT = [Buf("KiT%d" % i) for i in range(NT)]
                wiT = cv.take(NT * 8 * 4, F32, "p (s h) -> p s h", h=8)
                B_wiT = [Buf("wiT%d" % i) for i in range(NT)]
            if fox:
                Wf = cv.take(8 * 8 * 2, BF16, "p (k c) -> p k c", k=8)
                B_Wf = Buf("Wf")
                load_w(Wf[:], B_Wf, w_in_view(l, C_F, 8))
                spT = cv.take(NT * 8 * 4, F32, "p (s h) -> p s h", h=8)
                B_spT = Buf("spT")
                ztmp = cv.take(64, F32)
                B_z = Buf("ztmp")
            import os
            P1MODE = os.environ.get("P1MODE", "")
            if P1MODE == "loads":
                return
            for ci in range(int(os.environ.get("P1N", "8"))):
                h_, Bh = load_hc_dep(ci * 512, 512)
                for tt in range(4):
                    gt = ci * 4 + tt
                    if P1MODE == "v":
                        ps, B = proj_T(h_, Bh, tt * 128, W[1], B_W[1], 512)
                        S.add("vector", lambda e, ps=ps, gt=gt: e.tensor_copy(V[:, gt, :, 0:64], ps[:].rearrange("p (h d) -> p h d", d=64)),
                              reads=[B], writes=[B_V[gt]])
                        continue
                    if P1MODE == "knorope":
                        ps, B = proj_T(h_, Bh, tt * 128, W[0], B_W[0], 512)
                        sb, Bs = evac_T(ps, B, 512, rope_tt=None)
                        to_F(sb, Bs, KT[0:64, :, gt * 128:(gt + 1) * 128], B_KT[gt], G64, 64, eng="scalar")
                        continue
                    if P1MODE == "krope":
                        ps, B = proj_T(h_, Bh, tt * 128, W[0], B_W[0], 512)
                        sb, Bs = evac_T(ps, B, 512, rope_tt=gt)
                        continue
                    ps, B = proj_T(h_, Bh, tt * 128, W[0], B_W[0], 512)
                    sb, Bs = evac_T(ps, B, 512, rope_tt=(None if fox else gt))
                    if moba:
                        to_F(sb, Bs, KT[0:64, :, gt * 128:(gt + 1) * 128], B_KT[gt], G64, 64, eng="scalar")
                    else:
                        to_F(sb, Bs, KT[:, :, gt * 128:(gt + 1) * 128], B_KT[gt], G128, 128, eng="scalar")
                    ps, B = proj_T(h_, Bh, tt * 128, W[1], B_W[1], 512)
                    S.add("vector", lambda e, ps=ps, gt=gt: e.tensor_copy(V[:, gt, :, 0:64], ps[:].rearrange("p (h d) -> p h d", d=64)),
                          reads=[B], writes=[B_V[gt]])
                    if dsa:
                        ps, B = proj_T(h_, Bh, tt * 128, Wi, B_Wi, 72)
                        S.add("vector", lambda e, ps=ps, gt=gt: e.tensor_scalar(wiT[:, gt, :], ps[:, 64:72], float(8 ** -0.5 * 64 ** -0.5), None, ALU.mult),
                              reads=[B], writes=[B_wiT[gt]])
                        sb, Bs = evac_T(ps, B, 64, nheads=1, rope_tt=gt, dup=True)
                        to_F(sb, Bs, KiT[:, gt * 128:(gt + 1) * 128].unsqueeze(1), B_KiT[gt], [(0, 128)], 128, eng="scalar")
                    if fox:
                        ps, B = proj_T(h_, Bh, tt * 128, Wf, B_Wf, 8)
                        S.add("vector", lambda e, ps=ps: e.tensor_tensor(ztmp[:, 0:8], ps[:, 0:8], fb_t[:, l, :], ALU.add), reads=[B, B_cst], writes=[B_z])
                        S.add("scalar", lambda e: e.activation(ztmp[:, 0:8], ztmp[:, 0:8], AF.Exp, scale=-1.0), reads=[B_z], writes=[B_z])
                        S.add("scalar", lambda e, gt=gt: e.activation(spT[:, gt, :], ztmp[:, 0:8], AF.Ln, bias=eps_t[:, 1:2]), reads=[B_z, B_eps], writes=[B_spT])

            if stop == "p1":
                return
            if moba:
                kms = cv.take(8 * 16 * 4, F32, "p (h n) -> p h n", h=8)
                B_kms = Buf("kms")
                kmT = cv.take(8 * 16 * 2, BF16, "p (h n) -> p h n", h=8)
                B_kmT = Buf("kmT")
                for h in range(8):
                    S.add("vector", lambda e, h=h: e.tensor_reduce(kms[0:64, h, :], KT[0:64, h, :].rearrange("p (n s) -> p n s", s=256), AX.X, ALU.add),
                          reads=B_KT, writes=[B_kms])
                S.add("vector", lambda e: e.tensor_scalar(kmT[0:64], kms[0:64], 1.0 / 256.0, None, ALU.mult), reads=[B_kms], writes=[B_kmT])
                gw = cv.take(8 * 16 * 4, F32, "p (h n) -> p h n", h=8)
                B_gw = Buf("gw")
                m8 = cv.take(8 * 8 * 4, F32, "p (h n) -> p h n", h=8)
                B_m8 = Buf("m8")
                cmpb = cv.take(8 * 16 * 4, F32, "p (h n) -> p h n", h=8)
                B_cmp = Buf("cmp")
                nma = cv.take(8 * 80 * 2, BF16, "p (h n) -> p h n", h=8)
                B_nma = Buf("nma")
                S.add("vector", lambda e: e.memset(gw[:], NEGBIG), writes=[B_gw])
                S.add("vector", lambda e: e.memset(nma[:], 0.0), writes=[B_nma])
            if dsa:
                QiT = cv.take(4 * CW * 2, BF16, "p (h t) -> p h t", h=4)
                B_QiT = Buf("QiT")
                Isc = cv.take(S_LEN * 4, F32)
                B_I = Buf("I")
                negM = cv.take(NJ * S_LEN * 2, BF16, "p (j s) -> p j s", j=NJ)
                B_negM = [Buf("negM%d" % j) for j in range(NJ)]
                rl = [cv.take(2048, F32) for _ in range(2)]
                B_rl = [Buf("rl0"), Buf("rl1")]
                sst = cv.take(256, F32)
                B_ss = Buf("sst")
            if fox:
                spF = cv.take(S_LEN * 4, F32)
                B_spF = Buf("spF")
                cpF = cv.take(S_LEN * 4, F32)
                B_cpF = Buf("cpF")
                cpT = cv.take(NT * 8 * 4, F32, "p (s h) -> p s h", h=8)
                B_cpT = Buf("cpT")
                biasF = cv.take(NT * 8 * 4, F32, "p (s h) -> p s h", h=8)
                B_bF = Buf("biasF")
                refsb = cv.take(64, F32)
                B_ref = Buf("refsb")
                corrF = cv.take(512 * 2, BF16)
                B_corr = Buf("corrF")
                selh = cv.take(8 * 128 * 2, BF16, "p (h m) -> p h m", h=8)
                B_sel = Buf("selh")
                S.add("vector", lambda e: e.tensor_copy(selh[0:8], identb[0:8, 0:8].unsqueeze(2).to_broadcast([8, 8, 128])), reads=[B_ib], writes=[B_sel])
                for q4 in range(8):
                    ps, B = next_ps()
                    for j in range(4):
                        gt = q4 * 4 + j
                        S.add("tensor", lambda e, ps=ps, j=j, gt=gt: e.transpose(ps[0:8, j * 128:(j + 1) * 128], spT[:, gt, :], identf),
                              reads=[B_spT, B_cst], writes=[B])
                    S.add("vector", lambda e, ps=ps, q4=q4: e.tensor_copy(spF[0:8, q4 * 512:(q4 + 1) * 512], ps[0:8, :]), reads=[B], writes=[B_spF])
                S.add("vector", lambda e: e.tensor_tensor_scan(cpF[0:8, :], spF[0:8, :], spF[0:8, :], 0.0, ALU.add, ALU.max), reads=[B_spF], writes=[B_cpF])
                ps, B = next_ps()
                for gt in range(NT):
                    S.add("tensor", lambda e, ps=ps, gt=gt: e.transpose(ps[:, gt * 8:(gt + 1) * 8], cpF[0:8, gt * 128:(gt + 1) * 128], cst[0:8, K_ID:K_ID + 8]),
                          reads=[B_cpF, B_cst], writes=[B])
                S.add("vector", lambda e, ps=ps: e.tensor_copy(cpT[:].rearrange("p s h -> p (s h)"), ps[:, 0:256]), reads=[B], writes=[B_cpT])
                if debug:
                    B_dbg = Buf("dbg")
                    S.add("sync", lambda e: e.dma_start(out=dbgf[:, 0, :], in_=spT.rearrange("p s h -> p (s h)")), reads=[B_spT], writes=[B_dbg], dma=True)
                    S.add("sync", lambda e: e.dma_start(out=dbgf[:, 1, :], in_=cpT.rearrange("p s h -> p (s h)")), reads=[B_cpT], writes=[B_dbg], dma=True)
                    S.add("sync", lambda e: e.dma_start(out=dbgf[0:8, 2, :], in_=cpF[0:8, 0:256]), reads=[B_cpF], writes=[B_dbg], dma=True)

            load_w(W[0][:], B_W[0], w_in_view(l, cq, 512))
            load_w(W[1][:], B_W[1], w_in_view(l, cg, 512))
            if dsa:
                load_w(W[2][:], B_W[2], w_in_view(l, C_QI, 512))
            if moba:
                S.add("gpsimd", lambda e: e.memset(QT[64:80, :, :], 0.0), writes=[B_QTa])

            def attn_head(ci, h):
                nfull = ci * NJ
                po, B_po = next_acc()
                order = [(nfull + j, j) for j in range(NJ)] + [(s_, None) for s_ in range(nfull)]
                for bi, (s_, j) in enumerate(order):
                    c0 = 0 if j is None else 128 * j
                    pss, B_s = next_ps()
                    if moba:
                        kt = KT[0:80, h, s_ * 128:(s_ + 1) * 128]
                        qt = QT[0:80, h, c0:CW]
                    else:
                        r0 = (h % 2) * 64
                        kt = KT[r0:r0 + 64, h // 2, s_ * 128:(s_ + 1) * 128]
                        qt = QT[r0:r0 + 64, h // 2, c0:CW]
                    single = moba and (j is None)
                    S.add("tensor", lambda e, pss=pss, kt=kt, qt=qt, c0=c0, single=single: e.matmul(pss[:, c0:CW], kt, qt, start=True, stop=single),
                          reads=[B_KT[s_], B_KTa, B_QT, B_QTa], writes=[B_s])
                    if dsa:
                        j0 = 0 if j is None else j
                        for jj in range(j0, NJ):
                            S.add("tensor", lambda e, pss=pss, jj=jj, s_=s_: e.matmul(pss[:, jj * 128:(jj + 1) * 128], negM[:, jj, s_ * 128:(s_ + 1) * 128], identb[:],
                                                                                   start=False, stop=(jj == NJ - 1)),
                                  reads=[B_negM[jj], B_ib], writes=[B_s])
                    else:
                        if fox:
                            S.add("tensor", lambda e, pss=pss, c0=c0, fin=(j is None): e.matmul(pss[:, c0:CW], selh[0:8, h, :], corrF[0:8, c0:CW], start=False, stop=fin),
                                  reads=[B_sel, B_corr], writes=[B_s])
                        if j is not None:
                            S.add("tensor", lambda e, pss=pss, c0=c0: e.matmul(pss[:, c0:c0 + 128], identb[:], trib[:], start=False, stop=True),
                                  reads=[B_ib, B_tb], writes=[B_s])
                    pi = rr["pb"] % 4
                    rr["pb"] += 1
                    pb, B_pb = pbuf[pi], B_pbuf[pi]
                    if fox:
                        S.add("scalar", lambda e, pb=pb, pss=pss, c0=c0, s_=s_: e.activation(pb[:, c0:CW], pss[:, c0:CW], AF.Exp, bias=biasF[:, s_, h:h + 1], scale=0.125),
                              reads=[B_s, B_bF], writes=[B_pb])
                    else:
                        S.add("scalar", lambda e, pb=pb, pss=pss, c0=c0: e.activation(pb[:, c0:CW], pss[:, c0:CW], AF.Exp, scale=0.125),
                              reads=[B_s], writes=[B_pb])
                    S.add("tensor", lambda e, pb=pb, c0=c0, s_=s_, bi=bi: e.matmul(po[0:65, c0:CW], V[:, s_, h, :], pb[:, c0:CW],
                                                                                 start=(bi == 0), stop=(bi == len(order) - 1), skip_group_check=True),
                          reads=[B_pb, B_V[s_], B_Vones], writes=[B_po])
                S.add("vector", lambda e: e.reciprocal(rcf[64:65, 0:CW], po[64:65, 0:CW]), reads=[B_po], writes=[B_rcf])
                S.add("vector", lambda e: e.tensor_copy(rcb[64:65, 0:CW], rcf[64:65, 0:CW]), reads=[B_rcf], writes=[B_rcb])
                pbc, B_pbc = next_ps()
                S.add("tensor", lambda e: e.matmul(pbc[0:64, 0:CW], onesb[64:65, 0:64], rcb[64:65, 0:CW], start=True, stop=True),
                      reads=[B_rcb, B_ob], writes=[B_pbc])
                S.add("scalar", lambda e: e.copy(bc_sb[0:64, 0:CW], pbc[0:64, 0:CW]), reads=[B_pbc], writes=[B_bc])
                S.add("vector", lambda e: e.tensor_tensor(otmp[0:64, 0:CW], po[0:64, 0:CW], bc_sb[0:64, 0:CW], ALU.mult), reads=[B_po, B_bc], writes=[B_ot])
                S.add("gpsimd", lambda e: e.tensor_tensor(OG[0:64, h, :], otmp[0:64, 0:CW], SG[0:64, h, :], ALU.mult), reads=[B_ot, B_SG], writes=[B_OG])

            for ci in range(NCH):
                h_, Bh = load_hc_dep(ci * CW, CW)
                if fox:
                    nst = (ci + 1) * NJ
                    ps, B = next_ps()
                    S.add("tensor", lambda e, ps=ps, nst=nst: e.matmul(ps[:, 0:8], sel127f, cpT[:, nst - 1, :], start=True, stop=True),
                          reads=[B_cpT, B_cst], writes=[B])
                    S.add("vector", lambda e, ps=ps: e.tensor_copy(refsb[:, 0:8], ps[:, 0:8]), reads=[B], writes=[B_ref])
                    S.add("vector", lambda e, nst=nst: e.tensor_tensor(biasF[:, 0:nst, :], cpT[:, 0:nst, :], refsb[:, 0:8].unsqueeze(1).to_broadcast([128, nst, 8]), ALU.subtract),
                          reads=[B_cpT, B_ref], writes=[B_bF])
                    e_ = (ci + 1) * CW - 1
                    S.add("vector", lambda e, ci=ci, e_=e_: e.tensor_scalar(corrF[0:8, 0:CW], cpF[0:8, ci * CW:(ci + 1) * CW], cpF[0:8, e_:e_ + 1], -8.0, ALU.subtract, ALU.mult),
                          reads=[B_cpF], writes=[B_corr])
                    if debug and ci == 99:
                        S.add("sync", lambda e: e.dma_start(out=dbgf[:, 3, :], in_=biasF.rearrange("p s h -> p (s h)")), reads=[B_bF], writes=[B_dbg], dma=True)
                for tt in range(NJ):
                    gt = ci * NJ + tt
                    tc_ = slice(tt * 128, (tt + 1) * 128)
                    ps, B = proj_T(h_, Bh, tt * 128, W[0], B_W[0], 512)
                    sb, Bs = evac_T(ps, B, 512, rope_tt=(None if fox else gt))
                    if moba:
                        to_F(sb, Bs, QT[0:64, :, tc_], B_QT, G64, 64)
                    else:
                        to_F(sb, Bs, QT[:, :, tc_], B_QT, G128, 128)
                    ps, B = proj_T(h_, Bh, tt * 128, W[1], B_W[1], 512)
                    sb, Bs = evac_T(ps, B, 512, func=AF.Silu)
                    to_F(sb, Bs, SG[0:64, :, tc_], B_SG, G64, 64)
                    if moba:
                        own = gt // 2
                        if own >= 4:
                            ps, B = next_ps()
                            for h in range(8):
                                S.add("tensor", lambda e, ps=ps, h=h, tc_=tc_: e.matmul(ps[:, h * 16:(h + 1) * 16], QT[0:64, h, tc_], kmT[0:64, h, :], start=True, stop=True),
                                      reads=[B_QT, B_kmT], writes=[B])
                            S.add("vector", lambda e, ps=ps, own=own: e.tensor_copy(gw[:, :, 0:own], ps[:, 0:128].rearrange("p (h n) -> p h n", n=16)[:, :, 0:own]),
                                  reads=[B], writes=[B_gw])
                            for h in range(8):
                                S.add("vector", lambda e, h=h: e.max(m8[:, h, :], gw[:, h, :]), reads=[B_gw], writes=[B_m8])
                            S.add("vector", lambda e, own=own: e.tensor_tensor(cmpb[:, :, 0:own], gw[:, :, 0:own], m8[:, :, 2:3].to_broadcast([128, 8, own]), ALU.is_lt),
                                  reads=[B_gw, B_m8], writes=[B_cmp])
                            S.add("vector", lambda e, own=own: e.tensor_scalar(nma[:, :, 64:64 + own], cmpb[:, :, 0:own], NEG, None, ALU.mult), reads=[B_cmp], writes=[B_nma])
                            for h in range(8):
                                S.add("tensor", lambda e, h=h: e.transpose(PSB[0:80, h * 128:(h + 1) * 128], nma[:, h, :], identb[:]),
                                      reads=[B_nma, B_ib], writes=[B_PSB])
                            S.add("vector", lambda e, tc_=tc_: e.tensor_copy(QT[64:80, :, tc_], PSB[64:80, :].rearrange("p (h t) -> p h t", t=128)),
                                  reads=[B_PSB], writes=[B_QTa])
                        elif tt == 0 and ci > 0:
                            S.add("gpsimd", lambda e: e.memset(QT[64:80, :, :], 0.0), writes=[B_QTa])
                    if dsa:
                        ps, B = proj_T(h_, Bh, tt * 128, W[2], B_W[2], 512)
                        sb, Bs = evac_T(ps, B, 512, rope_tt=gt)
                        to_F(sb, Bs, QiT[:, :, tc_], B_QiT, G128, 128)
                if dsa:
                    for tt in range(NJ):
                        gt = ci * NJ + tt
                        tc_ = slice(tt * 128, (tt + 1) * 128)
                        L = (gt + 1) * 128
                        nsc = (L + 511) // 512
                        for sc in range(nsc):
                            ncol = min(512, L - sc * 512)
                            cs_ = slice(sc * 512, sc * 512 + ncol)
                            for h in range(8):
                                r0 = (h % 2) * 64
                                ps, B = next_ps()
                                S.add("tensor", lambda e, ps=ps, h=h, r0=r0, tc_=tc_, cs_=cs_, ncol=ncol: e.matmul(ps[:, 0:ncol], QiT[r0:r0 + 64, h // 2, tc_], KiT[r0:r0 + 64, cs_], start=True, stop=True),
                                      reads=[B_QiT] + B_KiT[sc * 4:sc * 4 + 4], writes=[B])
                                r_, B_r = rl[h % 2], B_rl[h % 2]
                                S.add("scalar", lambda e, r_=r_, ps=ps, ncol=ncol: e.activation(r_[:, 0:ncol], ps[:, 0:ncol], AF.Relu), reads=[B], writes=[B_r])
                                if h == 0:
                                    S.add("vector", lambda e, r_=r_, cs_=cs_, ncol=ncol, gt=gt: e.tensor_scalar(Isc[:, cs_], r_[:, 0:ncol], wiT[:, gt, 0:1], None, ALU.mult),
                                          reads=[B_r, B_wiT[gt]], writes=[B_I])
                                else:
                                    S.add("vector", lambda e, r_=r_, cs_=cs_, ncol=ncol, gt=gt, h=h: e.scalar_tensor_tensor(Isc[:, cs_], r_[:, 0:ncol], wiT[:, gt, h:h + 1], Isc[:, cs_], ALU.mult, ALU.add),
                                          reads=[B_r, B_wiT[gt], B_I], writes=[B_I])
                        if gt >= 2:
                            S.add("vector", lambda e, L=L: e.tensor_reduce(sst[:, 0:1], Isc[:, 0:L], AX.X, ALU.max, apply_absolute_value=True), reads=[B_I], writes=[B_ss])
                        S.add("vector", lambda e, L=L: e.tensor_tensor(Isc[:, L - 128:L], Isc[:, L - 128:L], trinegf, ALU.add), reads=[B_I, B_cst], writes=[B_I])
                        if gt >= 2:
                            S.add("vector", lambda e: e.tensor_scalar(sst[:, 8:8 + NIT], pow_t, sst[:, 0:1], None, ALU.mult), reads=[B_ss, B_cst], writes=[B_ss])
                            S.add("vector", lambda e: e.tensor_scalar(sst[:, 1:2], sst[:, 0:1], -1.0, None, ALU.mult), reads=[B_ss], writes=[B_ss])
                            for it in range(NIT):
                                S.add("vector", lambda e, it=it: e.tensor_tensor(sst[:, 2:3], sst[:, 1:2], sst[:, 8 + it:9 + it], ALU.add), reads=[B_ss], writes=[B_ss])
                                S.add("vector", lambda e, L=L, tt=tt: e.tensor_scalar(negM[:, tt, 0:L], Isc[:, 0:L], sst[:, 2:3], None, ALU.is_ge, ALU.add, accum_out=sst[:, 3:4]),
                                      reads=[B_I, B_ss], writes=[B_negM[tt], B_ss])
                                S.add("vector", lambda e, it=it: e.scalar_tensor_tensor(sst[:, 4:5], sst[:, 3:4], 255.5, sst[:, 8 + it:9 + it], ALU.is_ge, ALU.mult), reads=[B_ss], writes=[B_ss])
                                S.add("vector", lambda e: e.tensor_tensor(sst[:, 1:2], sst[:, 1:2], sst[:, 4:5], ALU.add), reads=[B_ss], writes=[B_ss])
                        else:
                            S.add("vector", lambda e: e.memset(sst[:, 1:2], -1.0e29), writes=[B_ss])
                        S.add("vector", lambda e, L=L, tt=tt: e.tensor_scalar(negM[:, tt, 0:L], Isc[:, 0:L], sst[:, 1:2], NEG, ALU.is_lt, ALU.mult), reads=[B_I, B_ss], writes=[B_negM[tt]])
                for h in range(8):
                    attn_head(ci, h)
                S.add("sync", lambda e, ci=ci: e.dma_start(out=og_d[n, :, ci * CW:(ci + 1) * CW].rearrange("(h d) t -> d h t", d=64), in_=OG[0:64, :, :]),
                      reads=[B_OG], writes=[B_og[n][(ci * CW) // 512]], dma=True)


        def phaseE(l, last):
            S.barrier()
            cv = Carver()
            wb = cv.take(3 * 4 * 1024 * 2, BF16, "p (n k c) -> p n k c", n=3, k=4)
            wg = cv.take(8 * 3072 * 2, BF16, "p (k c) -> p k c", k=8)
            wo = cv.take(8 * 1024 * 2, BF16, "p (k c) -> p k c", k=8)
            B_wb, B_wg, B_wo = Buf("wb"), Buf("wg"), Buf("wo")
            for n in range(3):
                load_w(wb[:, n, :, :], B_wb, wten("w_br%d" % l, [3, 512, D])[n].rearrange("(k p) c -> p k c", p=128))
            for q in range(3):
                load_w(wg[:, :, q * 1024:(q + 1) * 1024], B_wg, w_in_view(l, C_M + q * 1024, 1024))
            load_w(wo[:], B_wo, wten("w_out%d" % l, [D, D]).rearrange("(k p) c -> p k c", p=128))
            xc = cv.take(16384, F32, "p (a b) -> p a b", b=512)
            B_xc = Buf("xcE")
            ogp = [cv.take(4 * 512 * 2, BF16, "p (k t) -> p k t", k=4) for _ in range(3)]
            B_ogp = [Buf("ogp%d" % i) for i in range(3)]
            sgm = [cv.take(1024, BF16) for _ in range(3)]
            B_sgm = [Buf("sgm%d" % i) for i in range(3)]
            mt = [cv.take(2048, F32) for _ in range(2)]
            B_mt = [Buf("mt0"), Buf("mt1")]
            mg = cv.take(8 * 512 * 2, BF16, "p (k t) -> p k t", k=8)
            B_mg = Buf("mg")
            sq = cv.take(8192, BF16, "p (a b) -> p a b", b=512)
            B_sq = Buf("sqE")
            rs = cv.take(2048, F32)
            B_rs = Buf("rsE")
            if last:
                of = cv.take(16384, F32, "p (a b) -> p a b", b=512)
                B_of = Buf("of")
            for ci in range(8):
                cs_ = slice(ci * 512, (ci + 1) * 512)
                h_, Bh = load_hc_dep(ci * 512, 512)
                S.add("sync", lambda e, cs_=cs_: e.dma_start(out=xc, in_=xs[:, cs_].rearrange("(k p) t -> p k t", p=128)), reads=[B_xs[ci]], writes=[B_xc], dma=True)
                for n in range(3):
                    S.add("sync", lambda e, n=n, cs_=cs_: e.dma_start(out=ogp[n], in_=og_d[n, :, cs_].rearrange("(k p) t -> p k t", p=128)),
                          reads=[B_og[n][ci]], writes=[B_ogp[n]], dma=True)
                for dt in range(8):
                    dc = slice(dt * 128, (dt + 1) * 128)
                    for n in range(3):
                        py, B_py = next_ps()
                        for k in range(4):
                            S.add("tensor", lambda e, py=py, n=n, k=k, dc=dc: e.matmul(py[:], wb[:, n, k, dc], ogp[n][:, k, :], start=(k == 0), stop=(k == 3)),
                                  reads=[B_wb, B_ogp[n]], writes=[B_py])
                        pg, B_pg = next_ps()
                        for k in range(8):
                            S.add("tensor", lambda e, pg=pg, n=n, k=k, dt=dt, h_=h_: e.matmul(pg[:], wg[:, k, n * 1024 + dt * 128:n * 1024 + (dt + 1) * 128], h_[:, k, :], start=(k == 0), stop=(k == 7)),
                                  reads=[B_wg, Bh], writes=[B_pg])
                        S.add("scalar", lambda e, n=n, pg=pg: e.activation(sgm[n][:], pg[:], AF.Sigmoid), reads=[B_pg], writes=[B_sgm[n]])
                        mi = 0 if n == 0 else 1
                        S.add("vector", lambda e, py=py, n=n, mi=mi: e.tensor_tensor(mt[mi][:], py[:], sgm[n][:], ALU.mult), reads=[B_py, B_sgm[n]], writes=[B_mt[mi]])
                        if n == 1:
                            S.add("gpsimd", lambda e: e.tensor_tensor(mt[0][:], mt[0][:], mt[1][:], ALU.add), reads=[B_mt[0], B_mt[1]], writes=[B_mt[0]])
                        if n == 2:
                            S.add("gpsimd", lambda e, dt=dt: e.tensor_tensor(mg[:, dt, :], mt[0][:], mt[1][:], ALU.add), reads=[B_mt[0], B_mt[1]], writes=[B_mg])
                for dt in range(8):
                    po, B_po = next_ps()
                    for k in range(8):
                        S.add("tensor", lambda e, po=po, k=k, dt=dt: e.matmul(po[:], wo[:, k, dt * 128:(dt + 1) * 128], mg[:, k, :], start=(k == 0), stop=(k == 7)),
                              reads=[B_wo, B_mg], writes=[B_po])
                    S.add("vector", lambda e, po=po, dt=dt: e.tensor_tensor(xc[:, dt, :], xc[:, dt, :], po[:], ALU.add), reads=[B_po, B_xc], writes=[B_xc])
                if not last:
                    S.add("gpsimd", lambda e, cs_=cs_: e.dma_start(out=xs[:, cs_].rearrange("(k p) t -> p k t", p=128), in_=xc), reads=[B_xc], writes=[B_xs[ci]], dma=True)
                    ho, B_ho = hc[rr["hc"] % 2], B_hc[rr["hc"] % 2]
                    rr["hc"] += 1
                    norm_chunk(xc, B_xc, l + 1, sq, B_sq, rs, B_rs, ho, B_ho)
                    S.add("sync", lambda e, ho=ho, cs_=cs_: e.dma_start(out=hs[:, cs_].rearrange("(k p) t -> p k t", p=128), in_=ho[:]), reads=[B_ho], writes=[B_hs[ci]], dma=True)
                else:
                    norm_chunk(xc, B_xc, 2, sq, B_sq, rs, B_rs, of, B_of)
                    S.add("sync", lambda e, cs_=cs_: e.dma_start(out=outT[:, cs_].rearrange("(k p) t -> p k t", p=128), in_=of), reads=[B_of], writes=[B_final], dma=True)

        phase0()
        for l in range(n_layers):
            if stop == "p0":
                break
            for n in branches:
                branch(l, n)
            if stop in ("p1", "b"):
                break
            phaseE(l, last=(l == n_layers - 1))
        S.emit(nc, st, final_bufs=[B_final])
    return nc


def _consts(norm_gain, forget_bias, final_gain):
    c = np.zeros((128, NCST), np.float32)
    c[:, K_ID:K_ID + 128] = np.eye(128, dtype=np.float32)
    k = np.arange(128)[:, None]
    f = np.arange(128)[None, :]
    c[:, K_TRI:K_TRI + 128] = np.where(f < k, NEG, 0.0)
    c[:, K_TRN:K_TRN + 128] = np.where(f > k, NEGBIG, 0.0)
    c[127, K_SEL:K_SEL + 128] = 1.0
    inv_freq = np.power(np.float32(500000.0), -np.arange(0, 16, 2, dtype=np.float32) / np.float32(16)).astype(np.float32)
    pos = np.arange(S_LEN, dtype=np.float32)
    ang = (pos[:, None] * inv_freq[None, :]).astype(np.float32)
    cosv = np.cos(ang).astype(np.float32).reshape(NT, 128, 8).transpose(1, 0, 2)
    sinv = np.sin(ang).astype(np.float32).reshape(NT, 128, 8).transpose(1, 0, 2)
    c[:, K_COS:K_COS + 256] = cosv.reshape(128, 256)
    c[:, K_SIN:K_SIN + 256] = sinv.reshape(128, 256)
    c[:, K_POW:K_POW + NIT] = (2.0 ** -np.arange(NIT, dtype=np.float64)).astype(np.float32)[None, :]
    g = np.stack([norm_gain[0], norm_gain[1], final_gain], 0).astype(np.float32)
    c[:, K_GAIN:K_GAIN + 24] = g.reshape(3, 8, 128).transpose(2, 0, 1).reshape(128, 24)
    c[:, K_FB:K_FB + 16] = np.broadcast_to(forget_bias.astype(np.float32).reshape(1, 16), (128, 16))
    return c


def _host_inputs(x, norm_gain, w_in, forget_bias, w_branch, w_out, final_gain):
    cst = _consts(np.asarray(norm_gain), np.asarray(forget_bias), np.asarray(final_gain))
    oh = (np.arange(S_LEN)[None, :] // 256 == np.arange(16)[:, None]).astype(np.float32)
    w_in = np.ascontiguousarray(w_in, dtype=np.float32)
    w_branch = np.ascontiguousarray(w_branch, dtype=np.float32)
    w_out = np.ascontiguousarray(w_out, dtype=np.float32)
    maps = []
    for c in range(4):
        b = c % 4
        m = {"xT": np.ascontiguousarray(np.asarray(x[b], dtype=np.float32).T), "cst": cst, "onehot": oh}
        for l in range(DEPTH):
            m["w_in%d" % l] = w_in[l]
            m["w_br%d" % l] = w_branch[l]
            m["w_out%d" % l] = w_out[l]
        maps.append(m)
    return maps


def kernel(x, norm_gain, w_in, forget_bias, w_branch, w_out, final_gain):
    x = np.asarray(x)
    maps = _host_inputs(x, np.asarray(norm_gain), np.asarray(w_in), np.asarray(forget_bias), np.asarray(w_branch),
                        np.asarray(w_out), np.asarray(final_gain))
    nc = build_program()
    maps = [{k: m[k] for k in nc.used_inputs} for m in maps]
    res = run_bass_kernel_spmd(nc, maps, core_ids=list(range(4)))
    out = np.stack([np.ascontiguousarray(res.results[b]["outT"].T) for b in range(4)], 0)
    return out.astype(np.float32)
```

```python
import numpy as np
from contextlib import ExitStack
import concourse.bass as bass
import concourse.mybir as mybir
from concourse.bass_utils import run_bass_kernel_spmd

F32 = mybir.dt.float32
BF16 = mybir.dt.bfloat16
ALU = mybir.AluOpType
AF = mybir.ActivationFunctionType
AX = mybir.AxisListType

D = 1024
S_LEN = 4096
DEPTH = 2
DIN = 9808
NT = S_LEN // 128
NIT = 16
NEG = -30000.0
NEGBIG = -1.0e30
RMS_EPS = 1e-6
C_QA, C_KA, C_VA, C_GA = 0, 512, 1024, 1536
C_QB, C_KB, C_VB, C_GB = 2048, 2560, 3072, 3584
C_QC, C_KC, C_VC, C_GC = 4096, 4608, 5120, 5632
C_QI, C_KI, C_WI, C_F, C_M = 6144, 6656, 6720, 6728, 6736
K_ID, K_TRI, K_TRN, K_SEL, K_COS, K_SIN, K_POW, K_GAIN, K_FB = 0, 128, 256, 384, 512, 768, 1024, 1056, 1080
NCST = 1096


class Buf:
    __slots__ = ("name", "w", "r", "excl")

    def __init__(self, name, excl=False):
        self.name = name
        self.w = None
        self.r = []
        self.excl = excl


class Op:
    __slots__ = ("eng", "fn", "deps", "sig", "sem", "val", "dma")


ENGS = ("tensor", "vector", "scalar", "gpsimd", "sync")
DMA_POOL = 10


class Sched:
    def __init__(self):
        self.ops = {e: [] for e in ENGS}
        self.bar = None
        self.bar_seen = set()

    def add(self, eng, fn, reads=(), writes=(), dma=False):
        op = Op()
        op.eng = eng
        op.fn = fn
        op.dma = dma
        op.sig = dma
        op.sem = None
        op.val = 0
        deps = {}
        for b in reads:
            if b.w is not None:
                deps[id(b.w)] = b.w
            if b.excl:
                for r in b.r:
                    if r.eng != eng:
                        deps[id(r)] = r
        for b in writes:
            if b.w is not None:
                deps[id(b.w)] = b.w
            for r in b.r:
                deps[id(r)] = r
        if self.bar is not None and eng not in self.bar_seen:
            self.bar_seen.add(eng)
            for d in self.bar:
                deps[id(d)] = d
        for b in reads:
            b.r.append(op)
        for b in writes:
            b.w = op
            b.r = []
        dl = []
        for d in deps.values():
            if d is op:
                continue
            if (not d.dma) and (not dma) and d.eng == "tensor" and eng == "tensor":
                continue
            d.sig = True
            dl.append(d)
        op.deps = dl
        self.ops[eng].append(op)
        return op

    def barrier(self):
        deps = []
        for e in ENGS:
            got_c = False
            nd = 0
            for op in reversed(self.ops[e]):
                if op.dma:
                    if nd < DMA_POOL:
                        deps.append(op)
                        nd += 1
                elif not got_c:
                    got_c = True
                    deps.append(op)
                if got_c and nd >= DMA_POOL:
                    break
        self.bar = deps
        self.bar_seen = set()

    def emit(self, nc, stack, final_bufs=()):
        fin = []
        for b in final_bufs:
            if b.w is not None:
                b.w.sig = True
                fin.append(b.w)
        esem = {e: stack.enter_context(nc.semaphore("c_" + e)) for e in ENGS if e != "sync"}
        pools = {e: [stack.enter_context(nc.semaphore("d_%s_%d" % (e, i))) for i in range(DMA_POOL)]
                 for e in ("sync", "scalar", "gpsimd")}
        for e in ENGS:
            cnt = 0
            k = 0
            uses = [0] * DMA_POOL
            last = [None] * DMA_POOL
            for op in self.ops[e]:
                if op.dma:
                    j = k % DMA_POOL
                    k += 1
                    uses[j] += 1
                    op.sem = pools[e][j]
                    op.val = 16 * uses[j]
                    if last[j] is not None:
                        op.deps.append(last[j])
                    last[j] = op
                elif op.sig:
                    cnt += 1
                    op.sem = esem[e]
                    op.val = cnt
        block = stack.enter_context(nc.Block())

        def run(e, eng):
            known = {}
            for op in self.ops[e]:
                for d in op.deps:
                    key = d.sem.num
                    if known.get(key, 0) < d.val:
                        eng.wait_ge(d.sem, d.val)
                        known[key] = d.val
                ins = op.fn(eng)
                if op.dma:
                    ins.then_inc(op.sem, 16)
                elif op.sig:
                    ins.then_inc(op.sem, 1)
            if e == "sync":
                for d in fin:
                    if known.get(d.sem.num, 0) < d.val:
                        eng.wait_ge(d.sem, d.val)
                        known[d.sem.num] = d.val

        @block.tensor
        def _(eng):
            run("tensor", eng)

        @block.vector
        def _(eng):
            run("vector", eng)

        @block.scalar
        def _(eng):
            run("scalar", eng)

        @block.gpsimd
        def _(eng):
            run("gpsimd", eng)

        @block.sync
        def _(eng):
            run("sync", eng)


def build_program(n_layers=DEPTH, debug=False, branches=(0, 1, 2), stop=None):
    nc = bass.Bass("TRN2", target_bir_lowering=False)
    xT = nc.dram_tensor("xT", [D, S_LEN], F32, kind="ExternalInput").ap()
    used_inputs = ["xT", "cst", "onehot"]
    nc.used_inputs = used_inputs
    _wcache = {}

    def wten(name, shape):
        if name not in _wcache:
            _wcache[name] = nc.dram_tensor(name, shape, F32, kind="ExternalInput").ap()
            used_inputs.append(name)
        return _wcache[name]
    cst_d = nc.dram_tensor("cst", [128, NCST], F32, kind="ExternalInput").ap()
    oh_d = nc.dram_tensor("onehot", [16, S_LEN], F32, kind="ExternalInput").ap()
    outT = nc.dram_tensor("outT", [D, S_LEN], F32, kind="ExternalOutput").ap()
    dbg_kind = "ExternalOutput" if debug else "Internal"
    xs = nc.dram_tensor("xs", [D, S_LEN], F32, kind=dbg_kind).ap()
    hs = nc.dram_tensor("hs", [D, S_LEN], BF16, kind="Internal").ap()
    og_d = nc.dram_tensor("og", [3, 512, S_LEN], BF16, kind=dbg_kind).ap()
    dbgf = nc.dram_tensor("dbgf", [128, 4, 256], F32, kind="ExternalOutput").ap() if debug else None

    S = Sched()
    st = ExitStack()
    with st:
        def sbt(name, shape, dt):
            return st.enter_context(nc.sbuf_tensor(name, shape, dt))

        def pst(name, shape, dt):
            return st.enter_context(nc.psum_tensor(name, shape, dt))

        cst = sbt("cst_sb", [128, NCST], F32)
        identb = sbt("identb", [128, 128], BF16)
        trib = sbt("trib", [128, 128], BF16)
        onesb = sbt("onesb", [128, 128], BF16)
        hc = [sbt("hc%d" % i, [128, 8, 512], BF16) for i in range(2)]
        sbT = [sbt("sbT%d" % i, [128, 512], BF16) for i in range(2)]
        rtmp = [sbt("rtmp%d" % i, [128, 64], F32) for i in range(2)]
        pbuf = [sbt("pbuf%d" % i, [128, 512], BF16) for i in range(4)]
        ARENA_B = 168 * 1024
        arena = sbt("arena", [128, ARENA_B // 2], BF16)
        B_cst = Buf("cst")
        B_hc = [Buf("hc0"), Buf("hc1")]
        B_sbT = [Buf("sbT0"), Buf("sbT1")]
        B_rtmp = [Buf("rt0"), Buf("rt1")]
        B_pbuf = [Buf("pb%d" % i) for i in range(4)]
        B_final = Buf("final")
        identf = cst[:, K_ID:K_ID + 128]
        trinegf = cst[:, K_TRN:K_TRN + 128]
        sel127f = cst[:, K_SEL:K_SEL + 128]
        cos_t = cst[:, K_COS:K_COS + 256].rearrange("p (a b) -> p a b", b=8)
        sin_t = cst[:, K_SIN:K_SIN + 256].rearrange("p (a b) -> p a b", b=8)
        pow_t = cst[:, K_POW:K_POW + NIT]
        gain_t = cst[:, K_GAIN:K_GAIN + 24].rearrange("p (a b) -> p a b", b=8)
        fb_t = cst[:, K_FB:K_FB + 16].rearrange("p (a b) -> p a b", b=8)

        PS = [pst("ps%d" % i, [128, 512], F32) for i in range(7)]
        B_PS = [Buf("ps%d" % i, excl=True) for i in range(7)]
        PSB = pst("psb", [128, 1024], BF16)
        B_PSB = Buf("psb", excl=True)

        class Carver:
            def __init__(self):
                self.off = 0

            def take(self, nbytes, dt, pat=None, **kw):
                nbytes = (nbytes + 63) // 64 * 64
                a = arena[:, self.off // 2:(self.off + nbytes) // 2]
                self.off += nbytes
                assert self.off <= ARENA_B, ("arena overflow", self.off)
                if dt == F32:
                    a = a.bitcast(F32)
                if pat is not None:
                    a = a.rearrange(pat, **kw)
                return a

        S.add("sync", lambda e: e.dma_start(out=cst[:], in_=cst_d), writes=[B_cst], dma=True)
        B_ib, B_tb, B_ob = Buf("identb"), Buf("trib"), Buf("onesb")
        S.add("vector", lambda e: e.tensor_copy(identb[:], cst[:, K_ID:K_ID + 128]), reads=[B_cst], writes=[B_ib])
        S.add("vector", lambda e: e.tensor_copy(trib[:], cst[:, K_TRI:K_TRI + 128]), reads=[B_cst], writes=[B_tb])
        S.add("vector", lambda e: e.memset(onesb[:], 1.0), writes=[B_ob])

        rr = {"ps": 0, "acc": 0, "sbT": 0, "hc": 0, "pb": 0}

        def next_ps():
            i = rr["ps"] % 5
            rr["ps"] += 1
            return PS[i], B_PS[i]

        def next_acc():
            i = 5 + rr["acc"] % 2
            rr["acc"] += 1
            return PS[i], B_PS[i]

        def load_w(dst, B_dst, src_ap, eng="gpsimd"):
            S.add(eng, lambda e: e.dma_start(out=dst, in_=src_ap), writes=[B_dst], dma=True)

        def w_in_view(l, c0, n):
            return wten("w_in%d" % l, [D, DIN])[:, c0:c0 + n].rearrange("(k p) n -> p k n", p=128)

        def proj_T(h, B_h, tcol, w, B_w, ncols):
            ps, B = next_ps()
            for k in range(8):
                S.add("tensor", lambda e, k=k: e.matmul(ps[:, 0:ncols], h[:, k, tcol:tcol + 128], w[:, k, 0:ncols],
                                                        start=(k == 0), stop=(k == 7)),
                      reads=[B_h, B_w], writes=[B])
            return ps, B

        def evac_T(ps, B, ncols, func=None, nheads=8, rope_tt=None, dup=False):
            i = rr["sbT"] % 2
            rr["sbT"] += 1
            sb, Bs = sbT[i], B_sbT[i]
            if func is None:
                S.add("scalar", lambda e: e.copy(sb[:, 0:ncols], ps[:, 0:ncols]), reads=[B], writes=[Bs])
            else:
                S.add("scalar", lambda e: e.activation(sb[:, 0:ncols], ps[:, 0:ncols], func), reads=[B], writes=[Bs])
            if rope_tt is not None:
                p3 = ps[:, 0:ncols].rearrange("p (h d) -> p d h", d=64)
                s3 = sb[:, 0:ncols].rearrange("p (h d) -> p d h", d=64)
                cb = cos_t[:, rope_tt, :].unsqueeze(2).to_broadcast([128, 8, nheads])
                sn = sin_t[:, rope_tt, :].unsqueeze(2).to_broadcast([128, 8, nheads])
                ta = rtmp[0][:, 0:nheads * 8].rearrange("p (d h) -> p d h", d=8)
                tb = rtmp[1][:, 0:nheads * 8].rearrange("p (d h) -> p d h", d=8)
                x1, x2 = p3[:, 0:8, :], p3[:, 8:16, :]
                Ba, Bb = B_rtmp
                S.add("vector", lambda e: e.tensor_tensor(ta, x1, cb, ALU.mult), reads=[B, B_cst], writes=[Ba])
                S.add("vector", lambda e: e.tensor_tensor(tb, x2, sn, ALU.mult), reads=[B, B_cst], writes=[Bb])
                S.add("vector", lambda e: e.tensor_tensor(s3[:, 0:8, :], ta, tb, ALU.subtract), reads=[Ba, Bb], writes=[Bs])
                S.add("vector", lambda e: e.tensor_tensor(ta, x2, cb, ALU.mult), reads=[B, B_cst], writes=[Ba])
                S.add("vector", lambda e: e.tensor_tensor(tb, x1, sn, ALU.mult), reads=[B, B_cst], writes=[Bb])
                S.add("vector", lambda e: e.tensor_tensor(s3[:, 8:16, :], ta, tb, ALU.add), reads=[Ba, Bb], writes=[Bs])
            if dup:
                S.add("vector", lambda e: e.tensor_copy(sb[:, 64:128], sb[:, 0:64]), reads=[Bs], writes=[Bs])
            return sb, Bs

        def to_F(sb, Bs, dst3, B_dst, width, nparts, eng="vector"):
            ng = len(width)
            for g, (c0, cw) in enumerate(width):
                S.add("tensor", lambda e, g=g, c0=c0, cw=cw: e.transpose(PSB[0:cw, g * 128:(g + 1) * 128], sb[:, c0:c0 + cw], identb[:]),
                      reads=[Bs, B_ib], writes=[B_PSB])
            src = PSB[0:nparts, 0:ng * 128].rearrange("p (g t) -> p g t", t=128)
            if eng == "vector":
                S.add("vector", lambda e: e.tensor_copy(dst3, src), reads=[B_PSB], writes=[B_dst])
            else:
                S.add("scalar", lambda e: e.copy(dst3, src), reads=[B_PSB], writes=[B_dst])

        G64 = [(h * 64, 64) for h in range(8)]
        G128 = [(p * 128, 128) for p in range(4)]

        def norm_chunk(xc, B_xc, gi, sq, B_sq, rs, B_rs, out3, B_out):
            S.add("scalar", lambda e: e.activation(sq[:].rearrange("p a b -> p (a b)"), xc[:].rearrange("p a b -> p (a b)"), AF.Square),
                  reads=[B_xc], writes=[B_sq])
            ps, B = next_ps()
            for k in range(8):
                S.add("tensor", lambda e, k=k: e.matmul(ps[:], onesb[:], sq[:, k, :], start=(k == 0), stop=(k == 7)),
                      reads=[B_sq, B_ob], writes=[B])
            S.add("scalar", lambda e: e.activation(rs[:], ps[:], AF.Sqrt, bias=eps_t[:, 0:1], scale=1.0 / D), reads=[B, B_eps], writes=[B_rs])
            S.add("vector", lambda e: e.reciprocal(rs[:], rs[:]), reads=[B_rs], writes=[B_rs])
            for k in range(8):
                S.add("vector", lambda e, k=k: e.scalar_tensor_tensor(out3[:, k, :], xc[:, k, :], gain_t[:, gi, k:k + 1], rs[:],
                                                                      ALU.mult, ALU.mult),
                      reads=[B_xc, B_rs, B_cst], writes=[B_out])

        eps_t = sbt("eps_t", [128, 2], F32)
        B_eps = Buf("eps")
        S.add("vector", lambda e: e.memset(eps_t[:, 0:1], RMS_EPS), writes=[B_eps])
        S.add("vector", lambda e: e.memset(eps_t[:, 1:2], 1.0), writes=[B_eps])

        def phase0():
            cv = Carver()
            xc = [cv.take(16384, F32, "p (a b) -> p a b", b=512) for _ in range(2)]
            B_xc = [Buf("xc0"), Buf("xc1")]
            sq = cv.take(8192, BF16, "p (a b) -> p a b", b=512)
            B_sq = Buf("sq")
            rs = cv.take(2048, F32)
            B_rs = Buf("rs")
            import os
            for ci in range(int(os.environ.get('P0N', '8'))):
                x_, Bx = xc[ci % 2], B_xc[ci % 2]
                S.add("sync", lambda e, x_=x_, ci=ci: e.dma_start(out=x_, in_=xT[:, ci * 512:(ci + 1) * 512].rearrange("(k p) t -> p k t", p=128)),
                      writes=[Bx], dma=True)
                if os.environ.get('P0XS', '1') == '1':
                    S.add("gpsimd", lambda e, x_=x_, ci=ci: e.dma_start(out=xs[:, ci * 512:(ci + 1) * 512].rearrange("(k p) t -> p k t", p=128), in_=x_),
                          reads=[Bx], writes=[B_xs[ci]], dma=True)
                h_, Bh = hc[ci % 2], B_hc[ci % 2]
                norm_chunk(x_, Bx, 0, sq, B_sq, rs, B_rs, h_, Bh)
                S.add("sync", lambda e, h_=h_, ci=ci: e.dma_start(out=hs[:, ci * 512:(ci + 1) * 512].rearrange("(k p) t -> p k t", p=128), in_=h_[:]),
                      reads=[Bh], writes=[B_hs[ci]], dma=True)

        B_xs = [Buf("xs%d" % i) for i in range(8)]
        B_hs = [Buf("hs%d" % i) for i in range(8)]
        B_og = [[Buf("og%d_%d" % (n, i)) for i in range(8)] for n in range(3)]

        def hs_bufs(c0, width):
            return [B_hs[i] for i in range(c0 // 512, (c0 + width - 1) // 512 + 1)]

        def load_hc_dep(c0, width):
            i = rr["hc"] % 2
            rr["hc"] += 1
            h, B = hc[i], B_hc[i]
            S.add("sync", lambda e: e.dma_start(out=h[:, :, 0:width],
                                                in_=hs[:, c0:c0 + width].rearrange("(k p) t -> p k t", p=128)),
                  reads=hs_bufs(c0, width), writes=[B], dma=True)
            return h, B

        def branch(l, n):
            S.barrier()
            cv = Carver()
            moba, dsa, fox = (n == 0), (n == 1), (n == 2)
            CW = 256 if dsa else 512
            NJ = CW // 128
            NCH = S_LEN // CW
            cq, ck, cvv, cg = [(C_QA, C_KA, C_VA, C_GA), (C_QB, C_KB, C_VB, C_GB), (C_QC, C_KC, C_VC, C_GC)][n]
            if moba:
                KT = cv.take(8 * S_LEN * 2, BF16, "p (h s) -> p h s", h=8)
            else:
                KT = cv.take(4 * S_LEN * 2, BF16, "p (h s) -> p h s", h=4)
            B_KT = [Buf("KT%d" % i) for i in range(NT)]
            V = cv.take(NT * 8 * 65 * 2, BF16, "p (s h d) -> p s h d", s=NT, h=8)
            B_V = [Buf("V%d" % i) for i in range(NT)]
            B_Vones = Buf("Vones")
            W = [cv.take(8 * 512 * 2, BF16, "p (k c) -> p k c", k=8) for _ in range(3)]
            B_W = [Buf("W%d" % i) for i in range(3)]
            nqh = 8 if moba else 4
            QT = cv.take(nqh * CW * 2, BF16, "p (h t) -> p h t", h=nqh)
            B_QT = Buf("QT")
            B_QTa = Buf("QTaug")
            SG = cv.take(8 * CW * 2, BF16, "p (h t) -> p h t", h=8)
            B_SG = Buf("SG")
            OG = cv.take(8 * CW * 2, BF16, "p (h t) -> p h t", h=8)
            B_OG = Buf("OG")
            bc_sb = cv.take(CW * 4, F32)
            B_bc = Buf("bc")
            otmp = cv.take(CW * 4, F32)
            B_ot = Buf("otmp")
            rcb = cv.take(CW * 2, BF16)
            B_rcb = Buf("rcb")
            rcf = cv.take(CW * 4, F32)
            B_rcf = Buf("rcf")

            S.add("gpsimd", lambda e: e.memset(V.rearrange("p s h d -> p (s h) d")[:, :, 64:65], 1.0), writes=[B_Vones])

            load_w(W[0][:], B_W[0], w_in_view(l, ck, 512))
            load_w(W[1][:], B_W[1], w_in_view(l, cvv, 512))
            B_KTa = Buf("KTaug")
            if moba:
                for h in range(8):
                    S.add("gpsimd", lambda e, h=h: e.dma_start(out=KT[64:80, h, :], in_=oh_d), writes=[B_KTa], dma=True)
            if dsa:
                Wi = cv.take(8 * 80 * 2, BF16, "p (k c) -> p k c", k=8)
                B_Wi = Buf("Wi")
                load_w(Wi[:, :, 0:72], B_Wi, w_in_view(l, C_KI, 72))
                KiT = cv.take(S_LEN * 2, BF16)
                B_KiT = [Buf("KiT%d" % i) for i in range(NT)]
                wiT = cv.take(NT * 8 * 4, F32, "p (s h) -> p s h", h=8)
                B_wiT = [Buf("wiT%d" % i) for i in range(NT)]
            if fox:
                Wf = cv.take(8 * 8 * 2, BF16, "p (k c) -> p k c", k=8)
                B_Wf = Buf("Wf")
                load_w(Wf[:], B_Wf, w_in_view(l, C_F, 8))
                spT = cv.take(NT * 8 * 4, F32, "p (s h) -> p s h", h=8)
                B_spT = Buf("spT")
                ztmp = cv.take(64, F32)
                B_z = Buf("ztmp")
            import os
            P1MODE = os.environ.get("P1MODE", "")
            if P1MODE == "loads":
                return
            for ci in range(int(os.environ.get("P1N", "8"))):
                h_, Bh = load_hc_dep(ci * 512, 512)
                for tt in range(4):
                    gt = ci * 4 + tt
                    if P1MODE == "v":
                        ps, B = proj_T(h_, Bh, tt * 128, W[1], B_W[1], 512)
                        S.add("vector", lambda e, ps=ps, gt=gt: e.tensor_copy(V[:, gt, :, 0:64], ps[:].rearrange("p (h d) -> p h d", d=64)),
                              reads=[B], writes=[B_V[gt]])
                        continue
                    if P1MODE == "knorope":
                        ps, B = proj_T(h_, Bh, tt * 128, W[0], B_W[0], 512)
                        sb, Bs = evac_T(ps, B, 512, rope_tt=None)
                        to_F(sb, Bs, KT[0:64, :, gt * 128:(gt + 1) * 128], B_KT[gt], G64, 64, eng="scalar")
                        continue
                    if P1MODE == "krope":
                        ps, B = proj_T(h_, Bh, tt * 128, W[0], B_W[0], 512)
                        sb, Bs = evac_T(ps, B, 512, rope_tt=gt)
                        continue
                    ps, B = proj_T(h_, Bh, tt * 128, W[0], B_W[0], 512)
                    sb, Bs = evac_T(ps, B, 512, rope_tt=(None if fox else gt))
                    if moba:
                        to_F(sb, Bs, KT[0:64, :, gt * 128:(gt + 1) * 128], B_KT[gt], G64, 64, eng="scalar")
                    else:
                        to_F(sb, Bs, KT[:, :, gt * 128:(gt + 1) * 128], B_KT[gt], G128, 128, eng="scalar")
                    ps, B = proj_T(h_, Bh, tt * 128, W[1], B_W[1], 512)
                    S.add("vector", lambda e, ps=ps, gt=gt: e.tensor_copy(V[:, gt, :, 0:64], ps[:].rearrange("p (h d) -> p h d", d=64)),
                          reads=[B], writes=[B_V[gt]])
                    if dsa:
                        ps, B = proj_T(h_, Bh, tt * 128, Wi, B_Wi, 72)
                        S.add("vector", lambda e, ps=ps, gt=gt: e.tensor_scalar(wiT[:, gt, :], ps[:, 64:72], float(8 ** -0.5 * 64 ** -0.5), None, ALU.mult),
                              reads=[B], writes=[B_wiT[gt]])
                        sb, Bs = evac_T(ps, B, 64, nheads=1, rope_tt=gt, dup=True)
                        to_F(sb, Bs, KiT[:, gt * 128:(gt + 1) * 128].unsqueeze(1), B_KiT[gt], [(0, 128)], 128, eng="scalar")
                    if fox:
                        ps, B = proj_T(h_, Bh, tt * 128, Wf, B_Wf, 8)
                        S.add("vector", lambda e, ps=ps: e.tensor_tensor(ztmp[:, 0:8], ps[:, 0:8], fb_t[:, l, :], ALU.add), reads=[B, B_cst], writes=[B_z])
                        S.add("scalar", lambda e: e.activation(ztmp[:, 0:8], ztmp[:, 0:8], AF.Exp, scale=-1.0), reads=[B_z], writes=[B_z])
                        S.add("scalar", lambda e, gt=gt: e.activation(spT[:, gt, :], ztmp[:, 0:8], AF.Ln, bias=eps_t[:, 1:2]), reads=[B_z, B_eps], writes=[B_spT])

            if stop == "p1":
                return
            if moba:
                kms = cv.take(8 * 16 * 4, F32, "p (h n) -> p h n", h=8)
                B_kms = Buf("kms")
                kmT = cv.take(8 * 16 * 2, BF16, "p (h n) -> p h n", h=8)
                B_kmT = Buf("kmT")
                for h in range(8):
                    S.add("vector", lambda e, h=h: e.tensor_reduce(kms[0:64, h, :], KT[0:64, h, :].rearrange("p (n s) -> p n s", s=256), AX.X, ALU.add),
                          reads=B_KT, writes=[B_kms])
                S.add("vector", lambda e: e.tensor_scalar(kmT[0:64], kms[0:64], 1.0 / 256.0, None, ALU.mult), reads=[B_kms], writes=[B_kmT])
                gw = cv.take(8 * 16 * 4, F32, "p (h n) -> p h n", h=8)
                B_gw = Buf("gw")
                m8 = cv.take(8 * 8 * 4, F32, "p (h n) -> p h n", h=8)
                B_m8 = Buf("m8")
                cmpb = cv.take(8 * 16 * 4, F32, "p (h n) -> p h n", h=8)
                B_cmp = Buf("cmp")
                nma = cv.take(8 * 80 * 2, BF16, "p (h n) -> p h n", h=8)
                B_nma = Buf("nma")
                S.add("vector", lambda e: e.memset(gw[:], NEGBIG), writes=[B_gw])
                S.add("vector", lambda e: e.memset(nma[:], 0.0), writes=[B_nma])
            if dsa:
                QiT = cv.take(4 * CW * 2, BF16, "p (h t) -> p h t", h=4)
                B_QiT = Buf("QiT")
                Isc = cv.take(S_LEN * 4, F32)
                B_I = Buf("I")
                negM = cv.take(NJ * S_LEN * 2, BF16, "p (j s) -> p j s", j=NJ)
                B_negM = [Buf("negM%d" % j) for j in range(NJ)]
                rl = [cv.take(2048, F32) for _ in range(2)]
                B_rl = [Buf("rl0"), Buf("rl1")]
                sst = cv.take(256, F32)
                B_ss = Buf("sst")
            if fox:
                spF = cv.take(S_LEN * 4, F32)
                B_spF = Buf("spF")
                cpF = cv.take(S_LEN * 4, F32)
                B_cpF = Buf("cpF")
                cpT = cv.take(NT * 8 * 4, F32, "p (s h) -> p s h", h=8)
                B_cpT = Buf("cpT")
                biasF = cv.take(NT * 8 * 4, F32, "p (s h) -> p s h", h=8)
                B_bF = Buf("biasF")
                refsb = cv.take(64, F32)
                B_ref = Buf("refsb")
                corrF = cv.take(512 * 2, BF16)
                B_corr = Buf("corrF")
                selh = cv.take(8 * 128 * 2, BF16, "p (h m) -> p h m", h=8)
                B_sel = Buf("selh")
                S.add("vector", lambda e: e.tensor_copy(selh[0:8], identb[0:8, 0:8].unsqueeze(2).to_broadcast([8, 8, 128])), reads=[B_ib], writes=[B_sel])
                for q4 in range(8):
                    ps, B = next_ps()
                    for j in range(4):
                        gt = q4 * 4 + j
                        S.add("tensor", lambda e, ps=ps, j=j, gt=gt: e.transpose(ps[0:8, j * 128:(j + 1) * 128], spT[:, gt, :], identf),
                              reads=[B_spT, B_cst], writes=[B])
                    S.add("vector", lambda e, ps=ps, q4=q4: e.tensor_copy(spF[0:8, q4 * 512:(q4 + 1) * 512], ps[0:8, :]), reads=[B], writes=[B_spF])
                S.add("vector", lambda e: e.tensor_tensor_scan(cpF[0:8, :], spF[0:8, :], spF[0:8, :], 0.0, ALU.add, ALU.max), reads=[B_spF], writes=[B_cpF])
                ps, B = next_ps()
                for gt in range(NT):
                    S.add("tensor", lambda e, ps=ps, gt=gt: e.transpose(ps[:, gt * 8:(gt + 1) * 8], cpF[0:8, gt * 128:(gt + 1) * 128], cst[0:8, K_ID:K_ID + 8]),
                          reads=[B_cpF, B_cst], writes=[B])
                S.add("vector", lambda e, ps=ps: e.tensor_copy(cpT[:].rearrange("p s h -> p (s h)"), ps[:, 0:256]), reads=[B], writes=[B_cpT])
                if debug:
                    B_dbg = Buf("dbg")
                    S.add("sync", lambda e: e.dma_start(out=dbgf[:, 0, :], in_=spT.rearrange("p s h -> p (s h)")), reads=[B_spT], writes=[B_dbg], dma=True)
                    S.add("sync", lambda e: e.dma_start(out=dbgf[:, 1, :], in_=cpT.rearrange("p s h -> p (s h)")), reads=[B_cpT], writes=[B_dbg], dma=True)
                    S.add("sync", lambda e: e.dma_start(out=dbgf[0:8, 2, :], in_=cpF[0:8, 0:256]), reads=[B_cpF], writes=[B_dbg], dma=True)

            load_w(W[0][:], B_W[0], w_in_view(l, cq, 512))
            load_w(W[1][:], B_W[1], w_in_view(l, cg, 512))
            if dsa:
                load_w(W[2][:], B_W[2], w_in_view(l, C_QI, 512))
            if moba:
                S.add("gpsimd", lambda e: e.memset(QT[64:80, :, :], 0.0), writes=[B_QTa])

            def attn_head(ci, h):
                nfull = ci * NJ
                po, B_po = next_acc()
                order = [(nfull + j, j) for j in range(NJ)] + [(s_, None) for s_ in range(nfull)]
                for bi, (s_, j) in enumerate(order):
                    c0 = 0 if j is None else 128 * j
                    pss, B_s = next_ps()
                    if moba:
                        kt = KT[0:80, h, s_ * 128:(s_ + 1) * 128]
                        qt = QT[0:80, h, c0:CW]
                    else:
                        r0 = (h % 2) * 64
                        kt = KT[r0:r0 + 64, h // 2, s_ * 128:(s_ + 1) * 128]
                        qt = QT[r0:r0 + 64, h // 2, c0:CW]
                    single = moba and (j is None)
                    S.add("tensor", lambda e, pss=pss, kt=kt, qt=qt, c0=c0, single=single: e.matmul(pss[:, c0:CW], kt, qt, start=True, stop=single),
                          reads=[B_KT[s_], B_KTa, B_QT, B_QTa], writes=[B_s])
                    if dsa:
                        j0 = 0 if j is None else j
                        for jj in range(j0, NJ):
                            S.add("tensor", lambda e, pss=pss, jj=jj, s_=s_: e.matmul(pss[:, jj * 128:(jj + 1) * 128], negM[:, jj, s_ * 128:(s_ + 1) * 128], identb[:],
                                                                                   start=False, stop=(jj == NJ - 1)),
                                  reads=[B_negM[jj], B_ib], writes=[B_s])
                    else:
                        if fox:
                            S.add("tensor", lambda e, pss=pss, c0=c0, fin=(j is None): e.matmul(pss[:, c0:CW], selh[0:8, h, :], corrF[0:8, c0:CW], start=False, stop=fin),
                                  reads=[B_sel, B_corr], writes=[B_s])
                        if j is not None:
                            S.add("tensor", lambda e, pss=pss, c0=c0: e.matmul(pss[:, c0:c0 + 128], identb[:], trib[:], start=False, stop=True),
                                  reads=[B_ib, B_tb], writes=[B_s])
                    pi = rr["pb"] % 4
                    rr["pb"] += 1
                    pb, B_pb = pbuf[pi], B_pbuf[pi]
                    if fox:
                        S.add("scalar", lambda e, pb=pb, pss=pss, c0=c0, s_=s_: e.activation(pb[:, c0:CW], pss[:, c0:CW], AF.Exp, bias=biasF[:, s_, h:h + 1], scale=0.125),
                              reads=[B_s, B_bF], writes=[B_pb])
                    else:
                        S.add("scalar", lambda e, pb=pb, pss=pss, c0=c0: e.activation(pb[:, c0:CW], pss[:, c0:CW], AF.Exp, scale=0.125),
                              reads=[B_s], writes=[B_pb])
                    S.add("tensor", lambda e, pb=pb, c0=c0, s_=s_, bi=bi: e.matmul(po[0:65, c0:CW], V[:, s_, h, :], pb[:, c0:CW],
                                                                                 start=(bi == 0), stop=(bi == len(order) - 1), skip_group_check=True),
                          reads=[B_pb, B_V[s_], B_Vones], writes=[B_po])
                S.add("vector", lambda e: e.reciprocal(rcf[64:65, 0:CW], po[64:65, 0:CW]), reads=[B_po], writes=[B_rcf])
                S.add("vector", lambda e: e.tensor_copy(rcb[64:65, 0:CW], rcf[64:65, 0:CW]), reads=[B_rcf], writes=[B_rcb])
                pbc, B_pbc = next_ps()
                S.add("tensor", lambda e: e.matmul(pbc[0:64, 0:CW], onesb[64:65, 0:64], rcb[64:65, 0:CW], start=True, stop=True),
                      reads=[B_rcb, B_ob], writes=[B_pbc])
                S.add("scalar", lambda e: e.copy(bc_sb[0:64, 0:CW], pbc[0:64, 0:CW]), reads=[B_pbc], writes=[B_bc])
                S.add("vector", lambda e: e.tensor_tensor(otmp[0:64, 0:CW], po[0:64, 0:CW], bc_sb[0:64, 0:CW], ALU.mult), reads=[B_po, B_bc], writes=[B_ot])
                S.add("gpsimd", lambda e: e.tensor_tensor(OG[0:64, h, :], otmp[0:64, 0:CW], SG[0:64, h, :], ALU.mult), reads=[B_ot, B_SG], writes=[B_OG])

            for ci in range(NCH):
                h_, Bh = load_hc_dep(ci * CW, CW)
                if fox:
                    nst = (ci + 1) * NJ
                    ps, B = next_ps()
                    S.add("tensor", lambda e, ps=ps, nst=nst: e.matmul(ps[:, 0:8], sel127f, cpT[:, nst - 1, :], start=True, stop=True),
                          reads=[B_cpT, B_cst], writes=[B])
                    S.add("vector", lambda e, ps=ps: e.tensor_copy(refsb[:, 0:8], ps[:, 0:8]), reads=[B], writes=[B_ref])
                    S.add("vector", lambda e, nst=nst: e.tensor_tensor(biasF[:, 0:nst, :], cpT[:, 0:nst, :], refsb[:, 0:8].unsqueeze(1).to_broadcast([128, nst, 8]), ALU.subtract),
                          reads=[B_cpT, B_ref], writes=[B_bF])
                    e_ = (ci + 1) * CW - 1
                    S.add("vector", lambda e, ci=ci, e_=e_: e.tensor_scalar(corrF[0:8, 0:CW], cpF[0:8, ci * CW:(ci + 1) * CW], cpF[0:8, e_:e_ + 1], -8.0, ALU.subtract, ALU.mult),
                          reads=[B_cpF], writes=[B_corr])
                    if debug and ci == 99:
                        S.add("sync", lambda e: e.dma_start(out=dbgf[:, 3, :], in_=biasF.rearrange("p s h -> p (s h)")), reads=[B_bF], writes=[B_dbg], dma=True)
                for tt in range(NJ):
                    gt = ci * NJ + tt
                    tc_ = slice(tt * 128, (tt + 1) * 128)
                    ps, B = proj_T(h_, Bh, tt * 128, W[0], B_W[0], 512)
                    sb, Bs = evac_T(ps, B, 512, rope_tt=(None if fox else gt))
                    if moba:
                        to_F(sb, Bs, QT[0:64, :, tc_], B_QT, G64, 64)
                    else:
                        to_F(sb, Bs, QT[:, :, tc_], B_QT, G128, 128)
                    ps, B = proj_T(h_, Bh, tt * 128, W[1], B_W[1], 512)
                    sb, Bs = evac_T(ps, B, 512, func=AF.Silu)
                    to_F(sb, Bs, SG[0:64, :, tc_], B_SG, G64, 64)
                    if moba:
                        own = gt // 2
                        if own >= 4:
                            ps, B = next_ps()
                            for h in range(8):
                                S.add("tensor", lambda e, ps=ps, h=h, tc_=tc_: e.matmul(ps[:, h * 16:(h + 1) * 16], QT[0:64, h, tc_], kmT[0:64, h, :], start=True, stop=True),
                                      reads=[B_QT, B_kmT], writes=[B])
                            S.add("vector", lambda e, ps=ps, own=own: e.tensor_copy(gw[:, :, 0:own], ps[:, 0:128].rearrange("p (h n) -> p h n", n=16)[:, :, 0:own]),
                                  reads=[B], writes=[B_gw])
                            for h in range(8):
                                S.add("vector", lambda e, h=h: e.max(m8[:, h, :], gw[:, h, :]), reads=[B_gw], writes=[B_m8])
                            S.add("vector", lambda e, own=own: e.tensor_tensor(cmpb[:, :, 0:own], gw[:, :, 0:own], m8[:, :, 2:3].to_broadcast([128, 8, own]), ALU.is_lt),
                                  reads=[B_gw, B_m8], writes=[B_cmp])
                            S.add("vector", lambda e, own=own: e.tensor_scalar(nma[:, :, 64:64 + own], cmpb[:, :, 0:own], NEG, None, ALU.mult), reads=[B_cmp], writes=[B_nma])
                            for h in range(8):
                                S.add("tensor", lambda e, h=h: e.transpose(PSB[0:80, h * 128:(h + 1) * 128], nma[:, h, :], identb[:]),
                                      reads=[B_nma, B_ib], writes=[B_PSB])
                            S.add("vector", lambda e, tc_=tc_: e.tensor_copy(QT[64:80, :, tc_], PSB[64:80, :].rearrange("p (h t) -> p h t", t=128)),
                                  reads=[B_PSB], writes=[B_QTa])
                        elif tt == 0 and ci > 0:
                            S.add("gpsimd", lambda e: e.memset(QT[64:80, :, :], 0.0), writes=[B_QTa])
                    if dsa:
                        ps, B = proj_T(h_, Bh, tt * 128, W[2], B_W[2], 512)
                        sb, Bs = evac_T(ps, B, 512, rope_tt=gt)
                        to_F(sb, Bs, QiT[:, :, tc_], B_QiT, G128, 128)
                if dsa:
                    for tt in range(NJ):
                        gt = ci * NJ + tt
                        tc_ = slice(tt * 128, (tt + 1) * 128)
                        L = (gt + 1) * 128
                        nsc = (L + 511) // 512
                        for sc in range(nsc):
                            ncol = min(512, L - sc * 512)
                            cs_ = slice(sc * 512, sc * 512 + ncol)
                            for h in range(8):
                                r0 = (h % 2) * 64
                                ps, B = next_ps()
                                S.add("tensor", lambda e, ps=ps, h=h, r0=r0, tc_=tc_, cs_=cs_, ncol=ncol: e.matmul(ps[:, 0:ncol], QiT[r0:r0 + 64, h // 2, tc_], KiT[r0:r0 + 64, cs_], start=True, stop=True),
                                      reads=[B_QiT] + B_KiT[sc * 4:sc * 4 + 4], writes=[B])
                                r_, B_r = rl[h % 2], B_rl[h % 2]
                                S.add("scalar", lambda e, r_=r_, ps=ps, ncol=ncol: e.activation(r_[:, 0:ncol], ps[:, 0:ncol], AF.Relu), reads=[B], writes=[B_r])
                                if h == 0:
                                    S.add("vector", lambda e, r_=r_, cs_=cs_, ncol=ncol, gt=gt: e.tensor_scalar(Isc[:, cs_], r_[:, 0:ncol], wiT[:, gt, 0:1], None, ALU.mult),
                                          reads=[B_r, B_wiT[gt]], writes=[B_I])
                                else:
                                    S.add("vector", lambda e, r_=r_, cs_=cs_, ncol=ncol, gt=gt, h=h: e.scalar_tensor_tensor(Isc[:, cs_], r_[:, 0:ncol], wiT[:, gt, h:h + 1], Isc[:, cs_], ALU.mult, ALU.add),
                                          reads=[B_r, B_wiT[gt], B_I], writes=[B_I])
                        if gt >= 2:
                            S.add("vector", lambda e, L=L: e.tensor_reduce(sst[:, 0:1], Isc[:, 0:L], AX.X, ALU.max, apply_absolute_value=True), reads=[B_I], writes=[B_ss])
                        S.add("vector", lambda e, L=L: e.tensor_tensor(Isc[:, L - 128:L], Isc[:, L - 128:L], trinegf, ALU.add), reads=[B_I, B_cst], writes=[B_I])
                        if gt >= 2:
                            S.add("vector", lambda e: e.tensor_scalar(sst[:, 8:8 + NIT + 1], cst[:, K_POW:K_POW + NIT + 1], sst[:, 0:1], None, ALU.mult), reads=[B_ss, B_cst], writes=[B_ss])
                            S.add("vector", lambda e: e.memset(sst[:, 2:3], 0.0), writes=[B_ss])
                            for it in range(NIT):
                                S.add("vector", lambda e, L=L, tt=tt: e.tensor_scalar(negM[:, tt, 0:L], Isc[:, 0:L], sst[:, 2:3], None, ALU.is_ge, ALU.add, accum_out=sst[:, 3:4]),
                                      reads=[B_I, B_ss], writes=[B_negM[tt], B_ss])
                                S.add("vector", lambda e: e.tensor_scalar(sst[:, 4:5], sst[:, 3:4], 255.5, 0.5, ALU.is_ge, ALU.subtract), reads=[B_ss], writes=[B_ss])
                                S.add("vector", lambda e, it=it: e.scalar_tensor_tensor(sst[:, 2:3], sst[:, 4:5], sst[:, 8 + it:9 + it], sst[:, 2:3], ALU.mult, ALU.add), reads=[B_ss], writes=[B_ss])
                            S.add("vector", lambda e: e.tensor_tensor(sst[:, 1:2], sst[:, 2:3], sst[:, 8 + NIT:9 + NIT], ALU.subtract), reads=[B_ss], writes=[B_ss])
                        else:
                            S.add("vector", lambda e: e.memset(sst[:, 1:2], -1.0e29), writes=[B_ss])
                        S.add("vector", lambda e, L=L, tt=tt: e.tensor_scalar(negM[:, tt, 0:L], Isc[:, 0:L], sst[:, 1:2], NEG, ALU.is_lt, ALU.mult), reads=[B_I, B_ss], writes=[B_negM[tt]])
                for h in range(8):
                    attn_head(ci, h)
                S.add("sync", lambda e, ci=ci: e.dma_start(out=og_d[n, :, ci * CW:(ci + 1) * CW].rearrange("(h d) t -> d h t", d=64), in_=OG[0:64, :, :]),
                      reads=[B_OG], writes=[B_og[n][(ci * CW) // 512]], dma=True)


        def phaseE(l, last):
            S.barrier()
            cv = Carver()
            wb = cv.take(3 * 4 * 1024 * 2, BF16, "p (n k c) -> p n k c", n=3, k=4)
            wg = cv.take(8 * 3072 * 2, BF16, "p (k c) -> p k c", k=8)
            wo = cv.take(8 * 1024 * 2, BF16, "p (k c) -> p k c", k=8)
            B_wb, B_wg, B_wo = Buf("wb"), Buf("wg"), Buf("wo")
            for n in range(3):
                load_w(wb[:, n, :, :], B_wb, wten("w_br%d" % l, [3, 512, D])[n].rearrange("(k p) c -> p k c", p=128))
            for q in range(3):
                load_w(wg[:, :, q * 1024:(q + 1) * 1024], B_wg, w_in_view(l, C_M + q * 1024, 1024))
            load_w(wo[:], B_wo, wten("w_out%d" % l, [D, D]).rearrange("(k p) c -> p k c", p=128))
            xc = cv.take(16384, F32, "p (a b) -> p a b", b=512)
            B_xc = Buf("xcE")
            ogp = [cv.take(4 * 512 * 2, BF16, "p (k t) -> p k t", k=4) for _ in range(3)]
            B_ogp = [Buf("ogp%d" % i) for i in range(3)]
            sgm = [cv.take(1024, BF16) for _ in range(3)]
            B_sgm = [Buf("sgm%d" % i) for i in range(3)]
            mt = [cv.take(2048, F32) for _ in range(2)]
            B_mt = [Buf("mt0"), Buf("mt1")]
            mg = cv.take(8 * 512 * 2, BF16, "p (k t) -> p k t", k=8)
            B_mg = Buf("mg")
            sq = cv.take(8192, BF16, "p (a b) -> p a b", b=512)
            B_sq = Buf("sqE")
            rs = cv.take(2048, F32)
            B_rs = Buf("rsE")
            if last:
                of = cv.take(16384, F32, "p (a b) -> p a b", b=512)
                B_of = Buf("of")
            for ci in range(8):
                cs_ = slice(ci * 512, (ci + 1) * 512)
                h_, Bh = load_hc_dep(ci * 512, 512)
                S.add("sync", lambda e, cs_=cs_: e.dma_start(out=xc, in_=xs[:, cs_].rearrange("(k p) t -> p k t", p=128)), reads=[B_xs[ci]], writes=[B_xc], dma=True)
                for n in range(3):
                    S.add("sync", lambda e, n=n, cs_=cs_: e.dma_start(out=ogp[n], in_=og_d[n, :, cs_].rearrange("(k p) t -> p k t", p=128)),
                          reads=[B_og[n][ci]], writes=[B_ogp[n]], dma=True)
                for dt in range(8):
                    dc = slice(dt * 128, (dt + 1) * 128)
                    for n in range(3):
                        py, B_py = next_ps()
                        for k in range(4):
                            S.add("tensor", lambda e, py=py, n=n, k=k, dc=dc: e.matmul(py[:], wb[:, n, k, dc], ogp[n][:, k, :], start=(k == 0), stop=(k == 3)),
                                  reads=[B_wb, B_ogp[n]], writes=[B_py])
                        pg, B_pg = next_ps()
                        for k in range(8):
                            S.add("tensor", lambda e, pg=pg, n=n, k=k, dt=dt, h_=h_: e.matmul(pg[:], wg[:, k, n * 1024 + dt * 128:n * 1024 + (dt + 1) * 128], h_[:, k, :], start=(k == 0), stop=(k == 7)),
                                  reads=[B_wg, Bh], writes=[B_pg])
                        S.add("scalar", lambda e, n=n, pg=pg: e.activation(sgm[n][:], pg[:], AF.Sigmoid), reads=[B_pg], writes=[B_sgm[n]])
                        mi = 0 if n == 0 else 1
                        S.add("vector", lambda e, py=py, n=n, mi=mi: e.tensor_tensor(mt[mi][:], py[:], sgm[n][:], ALU.mult), reads=[B_py, B_sgm[n]], writes=[B_mt[mi]])
                        if n == 1:
                            S.add("gpsimd", lambda e: e.tensor_tensor(mt[0][:], mt[0][:], mt[1][:], ALU.add), reads=[B_mt[0], B_mt[1]], writes=[B_mt[0]])
                        if n == 2:
                            S.add("gpsimd", lambda e, dt=dt: e.tensor_tensor(mg[:, dt, :], mt[0][:], mt[1][:], ALU.add), reads=[B_mt[0], B_mt[1]], writes=[B_mg])
                for dt in range(8):
                    po, B_po = next_ps()
                    for k in range(8):
                        S.add("tensor", lambda e, po=po, k=k, dt=dt: e.matmul(po[:], wo[:, k, dt * 128:(dt + 1) * 128], mg[:, k, :], start=(k == 0), stop=(k == 7)),
                              reads=[B_wo, B_mg], writes=[B_po])
                    S.add("vector", lambda e, po=po, dt=dt: e.tensor_tensor(xc[:, dt, :], xc[:, dt, :], po[:], ALU.add), reads=[B_po, B_xc], writes=[B_xc])
                if not last:
                    S.add("gpsimd", lambda e, cs_=cs_: e.dma_start(out=xs[:, cs_].rearrange("(k p) t -> p k t", p=128), in_=xc), reads=[B_xc], writes=[B_xs[ci]], dma=True)
                    ho, B_ho = hc[rr["hc"] % 2], B_hc[rr["hc"] % 2]
                    rr["hc"] += 1
                    norm_chunk(xc, B_xc, l + 1, sq, B_sq, rs, B_rs, ho, B_ho)
                    S.add("sync", lambda e, ho=ho, cs_=cs_: e.dma_start(out=hs[:, cs_].rearrange("(k p) t -> p k t", p=128), in_=ho[:]), reads=[B_ho], writes=[B_hs[ci]], dma=True)
                else:
                    norm_chunk(xc, B_xc, 2, sq, B_sq, rs, B_rs, of, B_of)
                    S.add("sync", lambda e, cs_=cs_: e.dma_start(out=outT[:, cs_].rearrange("(k p) t -> p k t", p=128), in_=of), reads=[B_of], writes=[B_final], dma=True)

        phase0()
        for l in range(n_layers):
            if stop == "p0":
                break
            for n in branches:
                branch(l, n)
            if stop in ("p1", "b"):
                break
            phaseE(l, last=(l == n_layers - 1))
        S.emit(nc, st, final_bufs=[B_final])
    return nc


def _consts(norm_gain, forget_bias, final_gain):
    c = np.zeros((128, NCST), np.float32)
    c[:, K_ID:K_ID + 128] = np.eye(128, dtype=np.float32)
    k = np.arange(128)[:, None]
    f = np.arange(128)[None, :]
    c[:, K_TRI:K_TRI + 128] = np.where(f < k, NEG, 0.0)
    c[:, K_TRN:K_TRN + 128] = np.where(f > k, NEGBIG, 0.0)
    c[127, K_SEL:K_SEL + 128] = 1.0
    inv_freq = np.power(np.float32(500000.0), -np.arange(0, 16, 2, dtype=np.float32) / np.float32(16)).astype(np.float32)
    pos = np.arange(S_LEN, dtype=np.float32)
    ang = (pos[:, None] * inv_freq[None, :]).astype(np.float32)
    cosv = np.cos(ang).astype(np.float32).reshape(NT, 128, 8).transpose(1, 0, 2)
    sinv = np.sin(ang).astype(np.float32).reshape(NT, 128, 8).transpose(1, 0, 2)
    c[:, K_COS:K_COS + 256] = cosv.reshape(128, 256)
    c[:, K_SIN:K_SIN + 256] = sinv.reshape(128, 256)
    c[:, K_POW:K_POW + NIT + 1] = (2.0 ** -np.arange(NIT + 1, dtype=np.float64)).astype(np.float32)[None, :]
    g = np.stack([norm_gain[0], norm_gain[1], final_gain], 0).astype(np.float32)
    c[:, K_GAIN:K_GAIN + 24] = g.reshape(3, 8, 128).transpose(2, 0, 1).reshape(128, 24)
    c[:, K_FB:K_FB + 16] = np.broadcast_to(forget_bias.astype(np.float32).reshape(1, 16), (128, 16))
    return c


def _host_inputs(x, norm_gain, w_in, forget_bias, w_branch, w_out, final_gain):
    cst = _consts(np.asarray(norm_gain), np.asarray(forget_bias), np.asarray(final_gain))
    oh = (np.arange(S_LEN)[None, :] // 256 == np.arange(16)[:, None]).astype(np.float32)
    w_in = np.ascontiguousarray(w_in, dtype=np.float32)
    w_branch = np.ascontiguousarray(w_branch, dtype=np.float32)
    w_out = np.ascontiguousarray(w_out, dtype=np.float32)
    maps = []
    for c in range(4):
        b = c % 4
        m = {"xT": np.ascontiguousarray(np.asarray(x[b], dtype=np.float32).T), "cst": cst, "onehot": oh}
        for l in range(DEPTH):
            m["w_in%d" % l] = w_in[l]
            m["w_br%d" % l] = w_branch[l]
            m["w_out%d" % l] = w_out[l]
        maps.append(m)
    return maps


def kernel(x, norm_gain, w_in, forget_bias, w_branch, w_out, final_gain):
    x = np.asarray(x)
    maps = _host_inputs(x, np.asarray(norm_gain), np.asarray(w_in), np.asarray(forget_bias), np.asarray(w_branch),
                        np.asarray(w_out), np.asarray(final_gain))
    nc = build_program()
    maps = [{k: m[k] for k in nc.used_inputs} for m in maps]
    res = run_bass_kernel_spmd(nc, maps, core_ids=list(range(4)))
    out = np.stack([np.ascontiguousarray(res.results[b]["outT"].T) for b in range(4)], 0)
    return out.astype(np.float32)
```

```python
import numpy as np
from contextlib import ExitStack
import concourse.bass as bass
import concourse.mybir as mybir
from concourse.bass_utils import run_bass_kernel_spmd

F32 = mybir.dt.float32
BF16 = mybir.dt.bfloat16
ALU = mybir.AluOpType
AF = mybir.ActivationFunctionType
AX = mybir.AxisListType

D = 1024
S_LEN = 4096
DEPTH = 2
DIN = 9808
NT = S_LEN // 128
NIT = 16
NEG = -30000.0
NEGBIG = -1.0e30
RMS_EPS = 1e-6
C_QA, C_KA, C_VA, C_GA = 0, 512, 1024, 1536
C_QB, C_KB, C_VB, C_GB = 2048, 2560, 3072, 3584
C_QC, C_KC, C_VC, C_GC = 4096, 4608, 5120, 5632
C_QI, C_KI, C_WI, C_F, C_M = 6144, 6656, 6720, 6728, 6736
K_ID, K_TRI, K_TRN, K_SEL, K_COS, K_SIN, K_POW, K_GAIN, K_FB = 0, 128, 256, 384, 512, 768, 1024, 1056, 1080
NCST = 1096


class Buf:
    __slots__ = ("name", "w", "r", "excl")

    def __init__(self, name, excl=False):
        self.name = name
        self.w = None
        self.r = []
        self.excl = excl


class Op:
    __slots__ = ("eng", "fn", "deps", "sig", "sem", "val", "dma")


ENGS = ("tensor", "vector", "scalar", "gpsimd", "sync")
DMA_POOL = 10


class Sched:
    def __init__(self):
        self.ops = {e: [] for e in ENGS}
        self.bar = None
        self.bar_seen = set()

    def add(self, eng, fn, reads=(), writes=(), dma=False):
        op = Op()
        op.eng = eng
        op.fn = fn
        op.dma = dma
        op.sig = dma
        op.sem = None
        op.val = 0
        deps = {}
        for b in reads:
            if b.w is not None:
                deps[id(b.w)] = b.w
            if b.excl:
                for r in b.r:
                    if r.eng != eng:
                        deps[id(r)] = r
        for b in writes:
            if b.w is not None:
                deps[id(b.w)] = b.w
            for r in b.r:
                deps[id(r)] = r
        if self.bar is not None and eng not in self.bar_seen:
            self.bar_seen.add(eng)
            for d in self.bar:
                deps[id(d)] = d
        for b in reads:
            b.r.append(op)
        for b in writes:
            b.w = op
            b.r = []
        dl = []
        for d in deps.values():
            if d is op:
                continue
            if (not d.dma) and (not dma) and d.eng == "tensor" and eng == "tensor":
                continue
            d.sig = True
            dl.append(d)
        op.deps = dl
        self.ops[eng].append(op)
        return op

    def barrier(self):
        deps = []
        for e in ENGS:
            got_c = False
            nd = 0
            for op in reversed(self.ops[e]):
                if op.dma:
                    if nd < DMA_POOL:
                        deps.append(op)
                        nd += 1
                elif not got_c:
                    got_c = True
                    deps.append(op)
                if got_c and nd >= DMA_POOL:
                    break
        self.bar = deps
        self.bar_seen = set()

    def emit(self, nc, stack, final_bufs=()):
        fin = []
        for b in final_bufs:
            if b.w is not None:
                b.w.sig = True
                fin.append(b.w)
        esem = {e: stack.enter_context(nc.semaphore("c_" + e)) for e in ENGS if e != "sync"}
        pools = {e: [stack.enter_context(nc.semaphore("d_%s_%d" % (e, i))) for i in range(DMA_POOL)]
                 for e in ("sync", "scalar", "gpsimd")}
        for e in ENGS:
            cnt = 0
            k = 0
            uses = [0] * DMA_POOL
            last = [None] * DMA_POOL
            for op in self.ops[e]:
                if op.dma:
                    j = k % DMA_POOL
                    k += 1
                    uses[j] += 1
                    op.sem = pools[e][j]
                    op.val = 16 * uses[j]
                    if last[j] is not None:
                        op.deps.append(last[j])
                    last[j] = op
                elif op.sig:
                    cnt += 1
                    op.sem = esem[e]
                    op.val = cnt
        block = stack.enter_context(nc.Block())

        def run(e, eng):
            known = {}
            for op in self.ops[e]:
                for d in op.deps:
                    key = d.sem.num
                    if known.get(key, 0) < d.val:
                        eng.wait_ge(d.sem, d.val)
                        known[key] = d.val
                ins = op.fn(eng)
                if op.dma:
                    ins.then_inc(op.sem, 16)
                elif op.sig:
                    ins.then_inc(op.sem, 1)
            if e == "sync":
                for d in fin:
                    if known.get(d.sem.num, 0) < d.val:
                        eng.wait_ge(d.sem, d.val)
                        known[d.sem.num] = d.val

        @block.tensor
        def _(eng):
            run("tensor", eng)

        @block.vector
        def _(eng):
            run("vector", eng)

        @block.scalar
        def _(eng):
            run("scalar", eng)

        @block.gpsimd
        def _(eng):
            run("gpsimd", eng)

        @block.sync
        def _(eng):
            run("sync", eng)


def build_program(n_layers=DEPTH, debug=False, branches=(0, 1, 2), stop=None):
    nc = bass.Bass("TRN2", target_bir_lowering=False)
    xT = nc.dram_tensor("xT", [D, S_LEN], F32, kind="ExternalInput").ap()
    used_inputs = ["xT", "cst", "onehot"]
    nc.used_inputs = used_inputs
    _wcache = {}

    def wten(name, shape):
        if name not in _wcache:
            _wcache[name] = nc.dram_tensor(name, shape, F32, kind="ExternalInput").ap()
            used_inputs.append(name)
        return _wcache[name]
    cst_d = nc.dram_tensor("cst", [128, NCST], F32, kind="ExternalInput").ap()
    oh_d = nc.dram_tensor("onehot", [16, S_LEN], F32, kind="ExternalInput").ap()
    outT = nc.dram_tensor("outT", [D, S_LEN], F32, kind="ExternalOutput").ap()
    dbg_kind = "ExternalOutput" if debug else "Internal"
    xs = nc.dram_tensor("xs", [D, S_LEN], F32, kind=dbg_kind).ap()
    hs = nc.dram_tensor("hs", [D, S_LEN], BF16, kind="Internal").ap()
    og_d = nc.dram_tensor("og", [3, 512, S_LEN], BF16, kind=dbg_kind).ap()
    dbgf = nc.dram_tensor("dbgf", [128, 4, 256], F32, kind="ExternalOutput").ap() if debug else None

    S = Sched()
    st = ExitStack()
    with st:
        def sbt(name, shape, dt):
            return st.enter_context(nc.sbuf_tensor(name, shape, dt))

        def pst(name, shape, dt):
            return st.enter_context(nc.psum_tensor(name, shape, dt))

        cst = sbt("cst_sb", [128, NCST], F32)
        identb = sbt("identb", [128, 128], BF16)
        trib = sbt("trib", [128, 128], BF16)
        onesb = sbt("onesb", [128, 128], BF16)
        hc = [sbt("hc%d" % i, [128, 8, 512], BF16) for i in range(2)]
        sbT = [sbt("sbT%d" % i, [128, 512], BF16) for i in range(2)]
        rtmp = [sbt("rtmp%d" % i, [128, 64], F32) for i in range(2)]
        pbuf = [sbt("pbuf%d" % i, [128, 512], BF16) for i in range(4)]
        ARENA_B = 168 * 1024
        arena = sbt("arena", [128, ARENA_B // 2], BF16)
        B_cst = Buf("cst")
        B_hc = [Buf("hc0"), Buf("hc1")]
        B_sbT = [Buf("sbT0"), Buf("sbT1")]
        B_rtmp = [Buf("rt0"), Buf("rt1")]
        B_pbuf = [Buf("pb%d" % i) for i in range(4)]
        B_final = Buf("final")
        identf = cst[:, K_ID:K_ID + 128]
        trinegf = cst[:, K_TRN:K_TRN + 128]
        sel127f = cst[:, K_SEL:K_SEL + 128]
        cos_t = cst[:, K_COS:K_COS + 256].rearrange("p (a b) -> p a b", b=8)
        sin_t = cst[:, K_SIN:K_SIN + 256].rearrange("p (a b) -> p a b", b=8)
        pow_t = cst[:, K_POW:K_POW + NIT]
        gain_t = cst[:, K_GAIN:K_GAIN + 24].rearrange("p (a b) -> p a b", b=8)
        fb_t = cst[:, K_FB:K_FB + 16].rearrange("p (a b) -> p a b", b=8)

        PS = [pst("ps%d" % i, [128, 512], F32) for i in range(7)]
        B_PS = [Buf("ps%d" % i, excl=True) for i in range(7)]
        PSB = pst("psb", [128, 1024], BF16)
        B_PSB = Buf("psb", excl=True)

        class Carver:
            def __init__(self):
                self.off = 0

            def take(self, nbytes, dt, pat=None, **kw):
                nbytes = (nbytes + 63) // 64 * 64
                a = arena[:, self.off // 2:(self.off + nbytes) // 2]
                self.off += nbytes
                assert self.off <= ARENA_B, ("arena overflow", self.off)
                if dt == F32:
                    a = a.bitcast(F32)
                if pat is not None:
                    a = a.rearrange(pat, **kw)
                return a

        S.add("sync", lambda e: e.dma_start(out=cst[:], in_=cst_d), writes=[B_cst], dma=True)
        B_ib, B_tb, B_ob = Buf("identb"), Buf("trib"), Buf("onesb")
        S.add("vector", lambda e: e.tensor_copy(identb[:], cst[:, K_ID:K_ID + 128]), reads=[B_cst], writes=[B_ib])
        S.add("vector", lambda e: e.tensor_copy(trib[:], cst[:, K_TRI:K_TRI + 128]), reads=[B_cst], writes=[B_tb])
        S.add("vector", lambda e: e.memset(onesb[:], 1.0), writes=[B_ob])

        rr = {"ps": 0, "acc": 0, "sbT": 0, "hc": 0, "pb": 0}

        def next_ps():
            i = rr["ps"] % 5
            rr["ps"] += 1
            return PS[i], B_PS[i]

        def next_acc():
            i = 5 + rr["acc"] % 2
            rr["acc"] += 1
            return PS[i], B_PS[i]

        def load_w(dst, B_dst, src_ap, eng="gpsimd"):
            S.add(eng, lambda e: e.dma_start(out=dst, in_=src_ap), writes=[B_dst], dma=True)

        def w_in_view(l, c0, n):
            return wten("w_in%d" % l, [D, DIN])[:, c0:c0 + n].rearrange("(k p) n -> p k n", p=128)

        def proj_T(h, B_h, tcol, w, B_w, ncols):
            ps, B = next_ps()
            for k in range(8):
                S.add("tensor", lambda e, k=k: e.matmul(ps[:, 0:ncols], h[:, k, tcol:tcol + 128], w[:, k, 0:ncols],
                                                        start=(k == 0), stop=(k == 7)),
                      reads=[B_h, B_w], writes=[B])
            return ps, B

        def evac_T(ps, B, ncols, func=None, nheads=8, rope_tt=None, dup=False):
            i = rr["sbT"] % 2
            rr["sbT"] += 1
            sb, Bs = sbT[i], B_sbT[i]
            if func is None:
                S.add("scalar", lambda e: e.copy(sb[:, 0:ncols], ps[:, 0:ncols]), reads=[B], writes=[Bs])
            else:
                S.add("scalar", lambda e: e.activation(sb[:, 0:ncols], ps[:, 0:ncols], func), reads=[B], writes=[Bs])
            if rope_tt is not None:
                p3 = ps[:, 0:ncols].rearrange("p (h d) -> p d h", d=64)
                s3 = sb[:, 0:ncols].rearrange("p (h d) -> p d h", d=64)
                cb = cos_t[:, rope_tt, :].unsqueeze(2).to_broadcast([128, 8, nheads])
                sn = sin_t[:, rope_tt, :].unsqueeze(2).to_broadcast([128, 8, nheads])
                ta = rtmp[0][:, 0:nheads * 8].rearrange("p (d h) -> p d h", d=8)
                tb = rtmp[1][:, 0:nheads * 8].rearrange("p (d h) -> p d h", d=8)
                x1, x2 = p3[:, 0:8, :], p3[:, 8:16, :]
                Ba, Bb = B_rtmp
                S.add("vector", lambda e: e.tensor_tensor(ta, x1, cb, ALU.mult), reads=[B, B_cst], writes=[Ba])
                S.add("vector", lambda e: e.tensor_tensor(tb, x2, sn, ALU.mult), reads=[B, B_cst], writes=[Bb])
                S.add("vector", lambda e: e.tensor_tensor(s3[:, 0:8, :], ta, tb, ALU.subtract), reads=[Ba, Bb], writes=[Bs])
                S.add("vector", lambda e: e.tensor_tensor(ta, x2, cb, ALU.mult), reads=[B, B_cst], writes=[Ba])
                S.add("vector", lambda e: e.tensor_tensor(tb, x1, sn, ALU.mult), reads=[B, B_cst], writes=[Bb])
                S.add("vector", lambda e: e.tensor_tensor(s3[:, 8:16, :], ta, tb, ALU.add), reads=[Ba, Bb], writes=[Bs])
            if dup:
                S.add("vector", lambda e: e.tensor_copy(sb[:, 64:128], sb[:, 0:64]), reads=[Bs], writes=[Bs])
            return sb, Bs

        def to_F(sb, Bs, dst3, B_dst, width, nparts, eng="vector"):
            ng = len(width)
            for g, (c0, cw) in enumerate(width):
                S.add("tensor", lambda e, g=g, c0=c0, cw=cw: e.transpose(PSB[0:cw, g * 128:(g + 1) * 128], sb[:, c0:c0 + cw], identb[:]),
                      reads=[Bs, B_ib], writes=[B_PSB])
            src = PSB[0:nparts, 0:ng * 128].rearrange("p (g t) -> p g t", t=128)
            if eng == "vector":
                S.add("vector", lambda e: e.tensor_copy(dst3, src), reads=[B_PSB], writes=[B_dst])
            else:
                S.add("scalar", lambda e: e.copy(dst3, src), reads=[B_PSB], writes=[B_dst])

        G64 = [(h * 64, 64) for h in range(8)]
        G128 = [(p * 128, 128) for p in range(4)]

        def norm_chunk(xc, B_xc, gi, sq, B_sq, rs, B_rs, out3, B_out):
            S.add("scalar", lambda e: e.activation(sq[:].rearrange("p a b -> p (a b)"), xc[:].rearrange("p a b -> p (a b)"), AF.Square),
                  reads=[B_xc], writes=[B_sq])
            ps, B = next_ps()
            for k in range(8):
                S.add("tensor", lambda e, k=k: e.matmul(ps[:], onesb[:], sq[:, k, :], start=(k == 0), stop=(k == 7)),
                      reads=[B_sq, B_ob], writes=[B])
            S.add("scalar", lambda e: e.activation(rs[:], ps[:], AF.Sqrt, bias=eps_t[:, 0:1], scale=1.0 / D), reads=[B, B_eps], writes=[B_rs])
            S.add("vector", lambda e: e.reciprocal(rs[:], rs[:]), reads=[B_rs], writes=[B_rs])
            for k in range(8):
                S.add("vector", lambda e, k=k: e.scalar_tensor_tensor(out3[:, k, :], xc[:, k, :], gain_t[:, gi, k:k + 1], rs[:],
                                                                      ALU.mult, ALU.mult),
                      reads=[B_xc, B_rs, B_cst], writes=[B_out])

        eps_t = sbt("eps_t", [128, 2], F32)
        B_eps = Buf("eps")
        S.add("vector", lambda e: e.memset(eps_t[:, 0:1], RMS_EPS), writes=[B_eps])
        S.add("vector", lambda e: e.memset(eps_t[:, 1:2], 1.0), writes=[B_eps])

        def phase0():
            cv = Carver()
            xc = [cv.take(16384, F32, "p (a b) -> p a b", b=512) for _ in range(2)]
            B_xc = [Buf("xc0"), Buf("xc1")]
            sq = cv.take(8192, BF16, "p (a b) -> p a b", b=512)
            B_sq = Buf("sq")
            rs = cv.take(2048, F32)
            B_rs = Buf("rs")
            import os
            for ci in range(int(os.environ.get('P0N', '8'))):
                x_, Bx = xc[ci % 2], B_xc[ci % 2]
                S.add("sync", lambda e, x_=x_, ci=ci: e.dma_start(out=x_, in_=xT[:, ci * 512:(ci + 1) * 512].rearrange("(k p) t -> p k t", p=128)),
                      writes=[Bx], dma=True)
                if os.environ.get('P0XS', '1') == '1':
                    S.add("gpsimd", lambda e, x_=x_, ci=ci: e.dma_start(out=xs[:, ci * 512:(ci + 1) * 512].rearrange("(k p) t -> p k t", p=128), in_=x_),
                          reads=[Bx], writes=[B_xs[ci]], dma=True)
                h_, Bh = hc[ci % 2], B_hc[ci % 2]
                norm_chunk(x_, Bx, 0, sq, B_sq, rs, B_rs, h_, Bh)
                S.add("sync", lambda e, h_=h_, ci=ci: e.dma_start(out=hs[:, ci * 512:(ci + 1) * 512].rearrange("(k p) t -> p k t", p=128), in_=h_[:]),
                      reads=[Bh], writes=[B_hs[ci]], dma=True)

        B_xs = [Buf("xs%d" % i) for i in range(8)]
        B_hs = [Buf("hs%d" % i) for i in range(8)]
        B_og = [[Buf("og%d_%d" % (n, i)) for i in range(8)] for n in range(3)]

        def hs_bufs(c0, width):
            return [B_hs[i] for i in range(c0 // 512, (c0 + width - 1) // 512 + 1)]

        def load_hc_dep(c0, width):
            i = rr["hc"] % 2
            rr["hc"] += 1
            h, B = hc[i], B_hc[i]
            S.add("sync", lambda e: e.dma_start(out=h[:, :, 0:width],
                                                in_=hs[:, c0:c0 + width].rearrange("(k p) t -> p k t", p=128)),
                  reads=hs_bufs(c0, width), writes=[B], dma=True)
            return h, B

        def branch(l, n):
            S.barrier()
            cv = Carver()
            moba, dsa, fox = (n == 0), (n == 1), (n == 2)
            CW = 256 if dsa else 512
            NJ = CW // 128
            NCH = S_LEN // CW
            cq, ck, cvv, cg = [(C_QA, C_KA, C_VA, C_GA), (C_QB, C_KB, C_VB, C_GB), (C_QC, C_KC, C_VC, C_GC)][n]
            if moba:
                KT = cv.take(8 * S_LEN * 2, BF16, "p (h s) -> p h s", h=8)
            else:
                KT = cv.take(4 * S_LEN * 2, BF16, "p (h s) -> p h s", h=4)
            B_KT = [Buf("KT%d" % i) for i in range(NT)]
            V = cv.take(NT * 8 * 65 * 2, BF16, "p (s h d) -> p s h d", s=NT, h=8)
            B_V = [Buf("V%d" % i) for i in range(NT)]
            B_Vones = Buf("Vones")
            W = [cv.take(8 * 512 * 2, BF16, "p (k c) -> p k c", k=8) for _ in range(3)]
            B_W = [Buf("W%d" % i) for i in range(3)]
            nqh = 8 if moba else 4
            QT = cv.take(nqh * CW * 2, BF16, "p (h t) -> p h t", h=nqh)
            B_QT = Buf("QT")
            B_QTa = Buf("QTaug")
            SG = cv.take(8 * CW * 2, BF16, "p (h t) -> p h t", h=8)
            B_SG = Buf("SG")
            OG = cv.take(8 * CW * 2, BF16, "p (h t) -> p h t", h=8)
            B_OG = Buf("OG")
            bc_sb = cv.take(CW * 4, F32)
            B_bc = Buf("bc")
            otmp = cv.take(CW * 4, F32)
            B_ot = Buf("otmp")
            rcb = cv.take(CW * 2, BF16)
            B_rcb = Buf("rcb")
            rcf = cv.take(CW * 4, F32)
            B_rcf = Buf("rcf")

            S.add("gpsimd", lambda e: e.memset(V.rearrange("p s h d -> p (s h) d")[:, :, 64:65], 1.0), writes=[B_Vones])

            load_w(W[0][:], B_W[0], w_in_view(l, ck, 512))
            load_w(W[1][:], B_W[1], w_in_view(l, cvv, 512))
            B_KTa = Buf("KTaug")
            if moba:
                for h in range(8):
                    S.add("gpsimd", lambda e, h=h: e.dma_start(out=KT[64:80, h, :], in_=oh_d), writes=[B_KTa], dma=True)
            if dsa:
                Wi = cv.take(8 * 80 * 2, BF16, "p (k c) -> p k c", k=8)
                B_Wi = Buf("Wi")
                load_w(Wi[:, :, 0:72], B_Wi, w_in_view(l, C_KI, 72))
                KiT = cv.take(S_LEN * 2, BF16)
                B_KiT = [Buf("KiT%d" % i) for i in range(NT)]
                wiT = cv.take(NT * 8 * 4, F32, "p (s h) -> p s h", h=8)
                B_wiT = [Buf("wiT%d" % i) for i in range(NT)]
            if fox:
                Wf = cv.take(8 * 8 * 2, BF16, "p (k c) -> p k c", k=8)
                B_Wf = Buf("Wf")
                load_w(Wf[:], B_Wf, w_in_view(l, C_F, 8))
                spT = cv.take(NT * 8 * 4, F32, "p (s h) -> p s h", h=8)
                B_spT = Buf("spT")
                ztmp = cv.take(64, F32)
                B_z = Buf("ztmp")
            import os
            P1MODE = os.environ.get("P1MODE", "")
            if P1MODE == "loads":
                return
            for ci in range(int(os.environ.get("P1N", "8"))):
                h_, Bh = load_hc_dep(ci * 512, 512)
                for tt in range(4):
                    gt = ci * 4 + tt
                    if P1MODE == "v":
                        ps, B = proj_T(h_, Bh, tt * 128, W[1], B_W[1], 512)
                        S.add("vector", lambda e, ps=ps, gt=gt: e.tensor_copy(V[:, gt, :, 0:64], ps[:].rearrange("p (h d) -> p h d", d=64)),
                              reads=[B], writes=[B_V[gt]])
                        continue
                    if P1MODE == "knorope":
                        ps, B = proj_T(h_, Bh, tt * 128, W[0], B_W[0], 512)
                        sb, Bs = evac_T(ps, B, 512, rope_tt=None)
                        to_F(sb, Bs, KT[0:64, :, gt * 128:(gt + 1) * 128], B_KT[gt], G64, 64, eng="scalar")
                        continue
                    if P1MODE == "krope":
                        ps, B = proj_T(h_, Bh, tt * 128, W[0], B_W[0], 512)
                        sb, Bs = evac_T(ps, B, 512, rope_tt=gt)
                        continue
                    ps, B = proj_T(h_, Bh, tt * 128, W[0], B_W[0], 512)
                    sb, Bs = evac_T(ps, B, 512, rope_tt=(None if fox else gt))
                    if moba:
                        to_F(sb, Bs, KT[0:64, :, gt * 128:(gt + 1) * 128], B_KT[gt], G64, 64, eng="scalar")
                    else:
                        to_F(sb, Bs, KT[:, :, gt * 128:(gt + 1) * 128], B_KT[gt], G128, 128, eng="scalar")
                    ps, B = proj_T(h_, Bh, tt * 128, W[1], B_W[1], 512)
                    S.add("vector", lambda e, ps=ps, gt=gt: e.tensor_copy(V[:, gt, :, 0:64], ps[:].rearrange("p (h d) -> p h d", d=64)),
                          reads=[B], writes=[B_V[gt]])
                    if dsa:
                        ps, B = proj_T(h_, Bh, tt * 128, Wi, B_Wi, 72)
                        S.add("vector", lambda e, ps=ps, gt=gt: e.tensor_scalar(wiT[:, gt, :], ps[:, 64:72], float(8 ** -0.5 * 64 ** -0.5), None, ALU.mult),
                              reads=[B], writes=[B_wiT[gt]])
                        sb, Bs = evac_T(ps, B, 64, nheads=1, rope_tt=gt, dup=True)
                        to_F(sb, Bs, KiT[:, gt * 128:(gt + 1) * 128].unsqueeze(1), B_KiT[gt], [(0, 128)], 128, eng="scalar")
                    if fox:
                        ps, B = proj_T(h_, Bh, tt * 128, Wf, B_Wf, 8)
                        S.add("vector", lambda e, ps=ps: e.tensor_tensor(ztmp[:, 0:8], ps[:, 0:8], fb_t[:, l, :], ALU.add), reads=[B, B_cst], writes=[B_z])
                        S.add("scalar", lambda e: e.activation(ztmp[:, 0:8], ztmp[:, 0:8], AF.Exp, scale=-1.0), reads=[B_z], writes=[B_z])
                        S.add("scalar", lambda e, gt=gt: e.activation(spT[:, gt, :], ztmp[:, 0:8], AF.Ln, bias=eps_t[:, 1:2]), reads=[B_z, B_eps], writes=[B_spT])

            if stop == "p1":
                return
            if moba:
                kms = cv.take(8 * 16 * 4, F32, "p (h n) -> p h n", h=8)
                B_kms = Buf("kms")
                kmT = cv.take(8 * 16 * 2, BF16, "p (h n) -> p h n", h=8)
                B_kmT = Buf("kmT")
                for h in range(8):
                    S.add("vector", lambda e, h=h: e.tensor_reduce(kms[0:64, h, :], KT[0:64, h, :].rearrange("p (n s) -> p n s", s=256), AX.X, ALU.add),
                          reads=B_KT, writes=[B_kms])
                S.add("vector", lambda e: e.tensor_scalar(kmT[0:64], kms[0:64], 1.0 / 256.0, None, ALU.mult), reads=[B_kms], writes=[B_kmT])
                gw = cv.take(8 * 16 * 4, F32, "p (h n) -> p h n", h=8)
                B_gw = Buf("gw")
                m8 = cv.take(8 * 8 * 4, F32, "p (h n) -> p h n", h=8)
                B_m8 = Buf("m8")
                cmpb = cv.take(8 * 16 * 4, F32, "p (h n) -> p h n", h=8)
                B_cmp = Buf("cmp")
                nma = cv.take(8 * 80 * 2, BF16, "p (h n) -> p h n", h=8)
                B_nma = Buf("nma")
                S.add("vector", lambda e: e.memset(gw[:], NEGBIG), writes=[B_gw])
                S.add("vector", lambda e: e.memset(nma[:], 0.0), writes=[B_nma])
            if dsa:
                QiT = cv.take(4 * CW * 2, BF16, "p (h t) -> p h t", h=4)
                B_QiT = Buf("QiT")
                Isc = cv.take(S_LEN * 4, F32)
                B_I = Buf("I")
                negM = cv.take(NJ * S_LEN * 2, BF16, "p (j s) -> p j s", j=NJ)
                B_negM = [Buf("negM%d" % j) for j in range(NJ)]
                rl = [cv.take(2048, F32) for _ in range(2)]
                B_rl = [Buf("rl0"), Buf("rl1")]
                sst = cv.take(256, F32)
                B_ss = Buf("sst")
            if fox:
                spF = cv.take(S_LEN * 4, F32)
                B_spF = Buf("spF")
                cpF = cv.take(S_LEN * 4, F32)
                B_cpF = Buf("cpF")
                cpT = cv.take(NT * 8 * 4, F32, "p (s h) -> p s h", h=8)
                B_cpT = Buf("cpT")
                biasF = cv.take(NT * 8 * 4, F32, "p (s h) -> p s h", h=8)
                B_bF = Buf("biasF")
                refsb = cv.take(64, F32)
                B_ref = Buf("refsb")
                corrF = cv.take(512 * 2, BF16)
                B_corr = Buf("corrF")
                selh = cv.take(8 * 128 * 2, BF16, "p (h m) -> p h m", h=8)
                B_sel = Buf("selh")
                S.add("vector", lambda e: e.tensor_copy(selh[0:8], identb[0:8, 0:8].unsqueeze(2).to_broadcast([8, 8, 128])), reads=[B_ib], writes=[B_sel])
                for q4 in range(8):
                    ps, B = next_ps()
                    for j in range(4):
                        gt = q4 * 4 + j
                        S.add("tensor", lambda e, ps=ps, j=j, gt=gt: e.transpose(ps[0:8, j * 128:(j + 1) * 128], spT[:, gt, :], identf),
                              reads=[B_spT, B_cst], writes=[B])
                    S.add("vector", lambda e, ps=ps, q4=q4: e.tensor_copy(spF[0:8, q4 * 512:(q4 + 1) * 512], ps[0:8, :]), reads=[B], writes=[B_spF])
                S.add("vector", lambda e: e.tensor_tensor_scan(cpF[0:8, :], spF[0:8, :], spF[0:8, :], 0.0, ALU.add, ALU.max), reads=[B_spF], writes=[B_cpF])
                ps, B = next_ps()
                for gt in range(NT):
                    S.add("tensor", lambda e, ps=ps, gt=gt: e.transpose(ps[:, gt * 8:(gt + 1) * 8], cpF[0:8, gt * 128:(gt + 1) * 128], cst[0:8, K_ID:K_ID + 8]),
                          reads=[B_cpF, B_cst], writes=[B])
                S.add("vector", lambda e, ps=ps: e.tensor_copy(cpT[:].rearrange("p s h -> p (s h)"), ps[:, 0:256]), reads=[B], writes=[B_cpT])
                if debug:
                    B_dbg = Buf("dbg")
                    S.add("sync", lambda e: e.dma_start(out=dbgf[:, 0, :], in_=spT.rearrange("p s h -> p (s h)")), reads=[B_spT], writes=[B_dbg], dma=True)
                    S.add("sync", lambda e: e.dma_start(out=dbgf[:, 1, :], in_=cpT.rearrange("p s h -> p (s h)")), reads=[B_cpT], writes=[B_dbg], dma=True)
                    S.add("sync", lambda e: e.dma_start(out=dbgf[0:8, 2, :], in_=cpF[0:8, 0:256]), reads=[B_cpF], writes=[B_dbg], dma=True)

            load_w(W[0][:], B_W[0], w_in_view(l, cq, 512))
            load_w(W[1][:], B_W[1], w_in_view(l, cg, 512))
            if dsa:
                load_w(W[2][:], B_W[2], w_in_view(l, C_QI, 512))
            if moba:
                S.add("gpsimd", lambda e: e.memset(QT[64:80, :, :], 0.0), writes=[B_QTa])

            def attn_head(ci, h):
                nfull = ci * NJ
                po, B_po = next_acc()
                order = [(nfull + j, j) for j in range(NJ)] + [(s_, None) for s_ in range(nfull)]
                pending = []

                def emit_pv(pb, B_pb, c0, s_, bi):
                    S.add("tensor", lambda e: e.matmul(po[0:65, c0:CW], V[:, s_, h, :], pb[:, c0:CW],
                                                       start=(bi == 0), stop=(bi == len(order) - 1), skip_group_check=True),
                          reads=[B_pb, B_V[s_], B_Vones], writes=[B_po])
                for bi, (s_, j) in enumerate(order):
                    c0 = 0 if j is None else 128 * j
                    pss, B_s = next_ps()
                    if moba:
                        kt = KT[0:80, h, s_ * 128:(s_ + 1) * 128]
                        qt = QT[0:80, h, c0:CW]
                    else:
                        r0 = (h % 2) * 64
                        kt = KT[r0:r0 + 64, h // 2, s_ * 128:(s_ + 1) * 128]
                        qt = QT[r0:r0 + 64, h // 2, c0:CW]
                    single = moba and (j is None)
                    S.add("tensor", lambda e, pss=pss, kt=kt, qt=qt, c0=c0, single=single: e.matmul(pss[:, c0:CW], kt, qt, start=True, stop=single),
                          reads=[B_KT[s_], B_KTa, B_QT, B_QTa], writes=[B_s])
                    if dsa:
                        j0 = 0 if j is None else j
                        for jj in range(j0, NJ):
                            S.add("tensor", lambda e, pss=pss, jj=jj, s_=s_: e.matmul(pss[:, jj * 128:(jj + 1) * 128], negM[:, jj, s_ * 128:(s_ + 1) * 128], identb[:],
                                                                                   start=False, stop=(jj == NJ - 1)),
                                  reads=[B_negM[jj], B_ib], writes=[B_s])
                    else:
                        if fox:
                            S.add("tensor", lambda e, pss=pss, c0=c0, fin=(j is None): e.matmul(pss[:, c0:CW], selh[0:8, h, :], corrF[0:8, c0:CW], start=False, stop=fin),
                                  reads=[B_sel, B_corr], writes=[B_s])
                        if j is not None:
                            S.add("tensor", lambda e, pss=pss, c0=c0: e.matmul(pss[:, c0:c0 + 128], identb[:], trib[:], start=False, stop=True),
                                  reads=[B_ib, B_tb], writes=[B_s])
                    pi = rr["pb"] % 4
                    rr["pb"] += 1
                    pb, B_pb = pbuf[pi], B_pbuf[pi]
                    if fox:
                        S.add("scalar", lambda e, pb=pb, pss=pss, c0=c0, s_=s_: e.activation(pb[:, c0:CW], pss[:, c0:CW], AF.Exp, bias=biasF[:, s_, h:h + 1], scale=0.125),
                              reads=[B_s, B_bF], writes=[B_pb])
                    else:
                        S.add("scalar", lambda e, pb=pb, pss=pss, c0=c0: e.activation(pb[:, c0:CW], pss[:, c0:CW], AF.Exp, scale=0.125),
                              reads=[B_s], writes=[B_pb])
                    pending.append((pb, B_pb, c0, s_, bi))
                    if len(pending) > 2:
                        emit_pv(*pending.pop(0))
                while pending:
                    emit_pv(*pending.pop(0))
                S.add("vector", lambda e: e.reciprocal(rcf[64:65, 0:CW], po[64:65, 0:CW]), reads=[B_po], writes=[B_rcf])
                S.add("vector", lambda e: e.tensor_copy(rcb[64:65, 0:CW], rcf[64:65, 0:CW]), reads=[B_rcf], writes=[B_rcb])
                pbc, B_pbc = next_ps()
                S.add("tensor", lambda e: e.matmul(pbc[0:64, 0:CW], onesb[64:65, 0:64], rcb[64:65, 0:CW], start=True, stop=True),
                      reads=[B_rcb, B_ob], writes=[B_pbc])
                S.add("scalar", lambda e: e.copy(bc_sb[0:64, 0:CW], pbc[0:64, 0:CW]), reads=[B_pbc], writes=[B_bc])
                S.add("vector", lambda e: e.tensor_tensor(otmp[0:64, 0:CW], po[0:64, 0:CW], bc_sb[0:64, 0:CW], ALU.mult), reads=[B_po, B_bc], writes=[B_ot])
                S.add("gpsimd", lambda e: e.tensor_tensor(OG[0:64, h, :], otmp[0:64, 0:CW], SG[0:64, h, :], ALU.mult), reads=[B_ot, B_SG], writes=[B_OG])

            for ci in range(NCH):
                h_, Bh = load_hc_dep(ci * CW, CW)
                if fox:
                    nst = (ci + 1) * NJ
                    ps, B = next_ps()
                    S.add("tensor", lambda e, ps=ps, nst=nst: e.matmul(ps[:, 0:8], sel127f, cpT[:, nst - 1, :], start=True, stop=True),
                          reads=[B_cpT, B_cst], writes=[B])
                    S.add("vector", lambda e, ps=ps: e.tensor_copy(refsb[:, 0:8], ps[:, 0:8]), reads=[B], writes=[B_ref])
                    S.add("vector", lambda e, nst=nst: e.tensor_tensor(biasF[:, 0:nst, :], cpT[:, 0:nst, :], refsb[:, 0:8].unsqueeze(1).to_broadcast([128, nst, 8]), ALU.subtract),
                          reads=[B_cpT, B_ref], writes=[B_bF])
                    e_ = (ci + 1) * CW - 1
                    S.add("vector", lambda e, ci=ci, e_=e_: e.tensor_scalar(corrF[0:8, 0:CW], cpF[0:8, ci * CW:(ci + 1) * CW], cpF[0:8, e_:e_ + 1], -8.0, ALU.subtract, ALU.mult),
                          reads=[B_cpF], writes=[B_corr])
                    if debug and ci == 99:
                        S.add("sync", lambda e: e.dma_start(out=dbgf[:, 3, :], in_=biasF.rearrange("p s h -> p (s h)")), reads=[B_bF], writes=[B_dbg], dma=True)
                for tt in range(NJ):
                    gt = ci * NJ + tt
                    tc_ = slice(tt * 128, (tt + 1) * 128)
                    ps, B = proj_T(h_, Bh, tt * 128, W[0], B_W[0], 512)
                    sb, Bs = evac_T(ps, B, 512, rope_tt=(None if fox else gt))
                    if moba:
                        to_F(sb, Bs, QT[0:64, :, tc_], B_QT, G64, 64)
                    else:
                        to_F(sb, Bs, QT[:, :, tc_], B_QT, G128, 128)
                    ps, B = proj_T(h_, Bh, tt * 128, W[1], B_W[1], 512)
                    sb, Bs = evac_T(ps, B, 512, func=AF.Silu)
                    to_F(sb, Bs, SG[0:64, :, tc_], B_SG, G64, 64)
                    if moba:
                        own = gt // 2
                        if own >= 4:
                            ps, B = next_ps()
                            for h in range(8):
                                S.add("tensor", lambda e, ps=ps, h=h, tc_=tc_: e.matmul(ps[:, h * 16:(h + 1) * 16], QT[0:64, h, tc_], kmT[0:64, h, :], start=True, stop=True),
                                      reads=[B_QT, B_kmT], writes=[B])
                            S.add("vector", lambda e, ps=ps, own=own: e.tensor_copy(gw[:, :, 0:own], ps[:, 0:128].rearrange("p (h n) -> p h n", n=16)[:, :, 0:own]),
                                  reads=[B], writes=[B_gw])
                            for h in range(8):
                                S.add("vector", lambda e, h=h: e.max(m8[:, h, :], gw[:, h, :]), reads=[B_gw], writes=[B_m8])
                            S.add("vector", lambda e, own=own: e.tensor_tensor(cmpb[:, :, 0:own], gw[:, :, 0:own], m8[:, :, 2:3].to_broadcast([128, 8, own]), ALU.is_lt),
                                  reads=[B_gw, B_m8], writes=[B_cmp])
                            S.add("vector", lambda e, own=own: e.tensor_scalar(nma[:, :, 64:64 + own], cmpb[:, :, 0:own], NEG, None, ALU.mult), reads=[B_cmp], writes=[B_nma])
                            for h in range(8):
                                S.add("tensor", lambda e, h=h: e.transpose(PSB[0:80, h * 128:(h + 1) * 128], nma[:, h, :], identb[:]),
                                      reads=[B_nma, B_ib], writes=[B_PSB])
                            S.add("vector", lambda e, tc_=tc_: e.tensor_copy(QT[64:80, :, tc_], PSB[64:80, :].rearrange("p (h t) -> p h t", t=128)),
                                  reads=[B_PSB], writes=[B_QTa])
                        elif tt == 0 and ci > 0:
                            S.add("gpsimd", lambda e: e.memset(QT[64:80, :, :], 0.0), writes=[B_QTa])
                    if dsa:
                        ps, B = proj_T(h_, Bh, tt * 128, W[2], B_W[2], 512)
                        sb, Bs = evac_T(ps, B, 512, rope_tt=gt)
                        to_F(sb, Bs, QiT[:, :, tc_], B_QiT, G128, 128)
                if dsa:
                    for tt in range(NJ):
                        gt = ci * NJ + tt
                        tc_ = slice(tt * 128, (tt + 1) * 128)
                        L = (gt + 1) * 128
                        nsc = (L + 511) // 512
                        for sc in range(nsc):
                            ncol = min(512, L - sc * 512)
                            cs_ = slice(sc * 512, sc * 512 + ncol)
                            for h in range(8):
                                r0 = (h % 2) * 64
                                ps, B = next_ps()
                                S.add("tensor", lambda e, ps=ps, h=h, r0=r0, tc_=tc_, cs_=cs_, ncol=ncol: e.matmul(ps[:, 0:ncol], QiT[r0:r0 + 64, h // 2, tc_], KiT[r0:r0 + 64, cs_], start=True, stop=True),
                                      reads=[B_QiT] + B_KiT[sc * 4:sc * 4 + 4], writes=[B])
                                r_, B_r = rl[h % 2], B_rl[h % 2]
                                S.add("scalar", lambda e, r_=r_, ps=ps, ncol=ncol: e.activation(r_[:, 0:ncol], ps[:, 0:ncol], AF.Relu), reads=[B], writes=[B_r])
                                if h == 0:
                                    S.add("vector", lambda e, r_=r_, cs_=cs_, ncol=ncol, gt=gt: e.tensor_scalar(Isc[:, cs_], r_[:, 0:ncol], wiT[:, gt, 0:1], None, ALU.mult),
                                          reads=[B_r, B_wiT[gt]], writes=[B_I])
                                else:
                                    S.add("vector", lambda e, r_=r_, cs_=cs_, ncol=ncol, gt=gt, h=h: e.scalar_tensor_tensor(Isc[:, cs_], r_[:, 0:ncol], wiT[:, gt, h:h + 1], Isc[:, cs_], ALU.mult, ALU.add),
                                          reads=[B_r, B_wiT[gt], B_I], writes=[B_I])
                        if gt >= 2:
                            S.add("vector", lambda e, L=L: e.tensor_reduce(sst[:, 0:1], Isc[:, 0:L], AX.X, ALU.max, apply_absolute_value=True), reads=[B_I], writes=[B_ss])
                        S.add("vector", lambda e, L=L: e.tensor_tensor(Isc[:, L - 128:L], Isc[:, L - 128:L], trinegf, ALU.add), reads=[B_I, B_cst], writes=[B_I])
                        if gt >= 2:
                            S.add("vector", lambda e: e.tensor_scalar(sst[:, 8:8 + NIT + 1], cst[:, K_POW:K_POW + NIT + 1], sst[:, 0:1], None, ALU.mult), reads=[B_ss, B_cst], writes=[B_ss])
                            S.add("vector", lambda e: e.memset(sst[:, 2:3], 0.0), writes=[B_ss])
                            for it in range(NIT):
                                S.add("vector", lambda e, L=L, tt=tt: e.tensor_scalar(negM[:, tt, 0:L], Isc[:, 0:L], sst[:, 2:3], None, ALU.is_ge, ALU.add, accum_out=sst[:, 3:4]),
                                      reads=[B_I, B_ss], writes=[B_negM[tt], B_ss])
                                S.add("vector", lambda e: e.tensor_scalar(sst[:, 4:5], sst[:, 3:4], 255.5, 0.5, ALU.is_ge, ALU.subtract), reads=[B_ss], writes=[B_ss])
                                S.add("vector", lambda e, it=it: e.scalar_tensor_tensor(sst[:, 2:3], sst[:, 4:5], sst[:, 8 + it:9 + it], sst[:, 2:3], ALU.mult, ALU.add), reads=[B_ss], writes=[B_ss])
                            S.add("vector", lambda e: e.tensor_tensor(sst[:, 1:2], sst[:, 2:3], sst[:, 8 + NIT:9 + NIT], ALU.subtract), reads=[B_ss], writes=[B_ss])
                        else:
                            S.add("vector", lambda e: e.memset(sst[:, 1:2], -1.0e29), writes=[B_ss])
                        S.add("vector", lambda e, L=L, tt=tt: e.tensor_scalar(negM[:, tt, 0:L], Isc[:, 0:L], sst[:, 1:2], NEG, ALU.is_lt, ALU.mult), reads=[B_I, B_ss], writes=[B_negM[tt]])
                for h in range(8):
                    attn_head(ci, h)
                S.add("sync", lambda e, ci=ci: e.dma_start(out=og_d[n, :, ci * CW:(ci + 1) * CW].rearrange("(h d) t -> d h t", d=64), in_=OG[0:64, :, :]),
                      reads=[B_OG], writes=[B_og[n][(ci * CW) // 512]], dma=True)


        def phaseE(l, last):
            S.barrier()
            cv = Carver()
            wb = cv.take(3 * 4 * 1024 * 2, BF16, "p (n k c) -> p n k c", n=3, k=4)
            wg = cv.take(8 * 3072 * 2, BF16, "p (k c) -> p k c", k=8)
            wo = cv.take(8 * 1024 * 2, BF16, "p (k c) -> p k c", k=8)
            B_wb, B_wg, B_wo = Buf("wb"), Buf("wg"), Buf("wo")
            for n in range(3):
                load_w(wb[:, n, :, :], B_wb, wten("w_br%d" % l, [3, 512, D])[n].rearrange("(k p) c -> p k c", p=128))
            for q in range(3):
                load_w(wg[:, :, q * 1024:(q + 1) * 1024], B_wg, w_in_view(l, C_M + q * 1024, 1024))
            load_w(wo[:], B_wo, wten("w_out%d" % l, [D, D]).rearrange("(k p) c -> p k c", p=128))
            xc = cv.take(16384, F32, "p (a b) -> p a b", b=512)
            B_xc = Buf("xcE")
            ogp = [cv.take(4 * 512 * 2, BF16, "p (k t) -> p k t", k=4) for _ in range(3)]
            B_ogp = [Buf("ogp%d" % i) for i in range(3)]
            sgm = [cv.take(1024, BF16) for _ in range(3)]
            B_sgm = [Buf("sgm%d" % i) for i in range(3)]
            mt = [cv.take(2048, F32) for _ in range(2)]
            B_mt = [Buf("mt0"), Buf("mt1")]
            mg = cv.take(8 * 512 * 2, BF16, "p (k t) -> p k t", k=8)
            B_mg = Buf("mg")
            sq = cv.take(8192, BF16, "p (a b) -> p a b", b=512)
            B_sq = Buf("sqE")
            rs = cv.take(2048, F32)
            B_rs = Buf("rsE")
            if last:
                of = cv.take(16384, F32, "p (a b) -> p a b", b=512)
                B_of = Buf("of")
            for ci in range(8):
                cs_ = slice(ci * 512, (ci + 1) * 512)
                h_, Bh = load_hc_dep(ci * 512, 512)
                S.add("sync", lambda e, cs_=cs_: e.dma_start(out=xc, in_=xs[:, cs_].rearrange("(k p) t -> p k t", p=128)), reads=[B_xs[ci]], writes=[B_xc], dma=True)
                for n in range(3):
                    S.add("sync", lambda e, n=n, cs_=cs_: e.dma_start(out=ogp[n], in_=og_d[n, :, cs_].rearrange("(k p) t -> p k t", p=128)),
                          reads=[B_og[n][ci]], writes=[B_ogp[n]], dma=True)
                for dt in range(8):
                    dc = slice(dt * 128, (dt + 1) * 128)
                    for n in range(3):
                        py, B_py = next_ps()
                        for k in range(4):
                            S.add("tensor", lambda e, py=py, n=n, k=k, dc=dc: e.matmul(py[:], wb[:, n, k, dc], ogp[n][:, k, :], start=(k == 0), stop=(k == 3)),
                                  reads=[B_wb, B_ogp[n]], writes=[B_py])
                        pg, B_pg = next_ps()
                        for k in range(8):
                            S.add("tensor", lambda e, pg=pg, n=n, k=k, dt=dt, h_=h_: e.matmul(pg[:], wg[:, k, n * 1024 + dt * 128:n * 1024 + (dt + 1) * 128], h_[:, k, :], start=(k == 0), stop=(k == 7)),
                                  reads=[B_wg, Bh], writes=[B_pg])
                        S.add("scalar", lambda e, n=n, pg=pg: e.activation(sgm[n][:], pg[:], AF.Sigmoid), reads=[B_pg], writes=[B_sgm[n]])
                        mi = 0 if n == 0 else 1
                        S.add("vector", lambda e, py=py, n=n, mi=mi: e.tensor_tensor(mt[mi][:], py[:], sgm[n][:], ALU.mult), reads=[B_py, B_sgm[n]], writes=[B_mt[mi]])
                        if n == 1:
                            S.add("gpsimd", lambda e: e.tensor_tensor(mt[0][:], mt[0][:], mt[1][:], ALU.add), reads=[B_mt[0], B_mt[1]], writes=[B_mt[0]])
                        if n == 2:
                            S.add("gpsimd", lambda e, dt=dt: e.tensor_tensor(mg[:, dt, :], mt[0][:], mt[1][:], ALU.add), reads=[B_mt[0], B_mt[1]], writes=[B_mg])
                for dt in range(8):
                    po, B_po = next_ps()
                    for k in range(8):
                        S.add("tensor", lambda e, po=po, k=k, dt=dt: e.matmul(po[:], wo[:, k, dt * 128:(dt + 1) * 128], mg[:, k, :], start=(k == 0), stop=(k == 7)),
                              reads=[B_wo, B_mg], writes=[B_po])
                    S.add("vector", lambda e, po=po, dt=dt: e.tensor_tensor(xc[:, dt, :], xc[:, dt, :], po[:], ALU.add), reads=[B_po, B_xc], writes=[B_xc])
                if not last:
                    S.add("gpsimd", lambda e, cs_=cs_: e.dma_start(out=xs[:, cs_].rearrange("(k p) t -> p k t", p=128), in_=xc), reads=[B_xc], writes=[B_xs[ci]], dma=True)
                    ho, B_ho = hc[rr["hc"] % 2], B_hc[rr["hc"] % 2]
                    rr["hc"] += 1
                    norm_chunk(xc, B_xc, l + 1, sq, B_sq, rs, B_rs, ho, B_ho)
                    S.add("sync", lambda e, ho=ho, cs_=cs_: e.dma_start(out=hs[:, cs_].rearrange("(k p) t -> p k t", p=128), in_=ho[:]), reads=[B_ho], writes=[B_hs[ci]], dma=True)
                else:
                    norm_chunk(xc, B_xc, 2, sq, B_sq, rs, B_rs, of, B_of)
                    S.add("sync", lambda e, cs_=cs_: e.dma_start(out=outT[:, cs_].rearrange("(k p) t -> p k t", p=128), in_=of), reads=[B_of], writes=[B_final], dma=True)

        phase0()
        for l in range(n_layers):
            if stop == "p0":
                break
            for n in branches:
                branch(l, n)
            if stop in ("p1", "b"):
                break
            phaseE(l, last=(l == n_layers - 1))
        S.emit(nc, st, final_bufs=[B_final])
    return nc


def _consts(norm_gain, forget_bias, final_gain):
    c = np.zeros((128, NCST), np.float32)
    c[:, K_ID:K_ID + 128] = np.eye(128, dtype=np.float32)
    k = np.arange(128)[:, None]
    f = np.arange(128)[None, :]
    c[:, K_TRI:K_TRI + 128] = np.where(f < k, NEG, 0.0)
    c[:, K_TRN:K_TRN + 128] = np.where(f > k, NEGBIG, 0.0)
    c[127, K_SEL:K_SEL + 128] = 1.0
    inv_freq = np.power(np.float32(500000.0), -np.arange(0, 16, 2, dtype=np.float32) / np.float32(16)).astype(np.float32)
    pos = np.arange(S_LEN, dtype=np.float32)
    ang = (pos[:, None] * inv_freq[None, :]).astype(np.float32)
    cosv = np.cos(ang).astype(np.float32).reshape(NT, 128, 8).transpose(1, 0, 2)
    sinv = np.sin(ang).astype(np.float32).reshape(NT, 128, 8).transpose(1, 0, 2)
    c[:, K_COS:K_COS + 256] = cosv.reshape(128, 256)
    c[:, K_SIN:K_SIN + 256] = sinv.reshape(128, 256)
    c[:, K_POW:K_POW + NIT + 1] = (2.0 ** -np.arange(NIT + 1, dtype=np.float64)).astype(np.float32)[None, :]
    g = np.stack([norm_gain[0], norm_gain[1], final_gain], 0).astype(np.float32)
    c[:, K_GAIN:K_GAIN + 24] = g.reshape(3, 8, 128).transpose(2, 0, 1).reshape(128, 24)
    c[:, K_FB:K_FB + 16] = np.broadcast_to(forget_bias.astype(np.float32).reshape(1, 16), (128, 16))
    return c


def _host_inputs(x, norm_gain, w_in, forget_bias, w_branch, w_out, final_gain):
    cst = _consts(np.asarray(norm_gain), np.asarray(forget_bias), np.asarray(final_gain))
    oh = (np.arange(S_LEN)[None, :] // 256 == np.arange(16)[:, None]).astype(np.float32)
    w_in = np.ascontiguousarray(w_in, dtype=np.float32)
    w_branch = np.ascontiguousarray(w_branch, dtype=np.float32)
    w_out = np.ascontiguousarray(w_out, dtype=np.float32)
    maps = []
    for c in range(4):
        b = c % 4
        m = {"xT": np.ascontiguousarray(np.asarray(x[b], dtype=np.float32).T), "cst": cst, "onehot": oh}
        for l in range(DEPTH):
            m["w_in%d" % l] = w_in[l]
            m["w_br%d" % l] = w_branch[l]
            m["w_out%d" % l] = w_out[l]
        maps.append(m)
    return maps


def kernel(x, norm_gain, w_in, forget_bias, w_branch, w_out, final_gain):
    x = np.asarray(x)
    maps = _host_inputs(x, np.asarray(norm_gain), np.asarray(w_in), np.asarray(forget_bias), np.asarray(w_branch),
                        np.asarray(w_out), np.asarray(final_gain))
    nc = build_program()
    maps = [{k: m[k] for k in nc.used_inputs} for m in maps]
    res = run_bass_kernel_spmd(nc, maps, core_ids=list(range(4)))
    out = np.stack([np.ascontiguousarray(res.results[b]["outT"].T) for b in range(4)], 0)
    return out.astype(np.float32)
```
